# Optimizing a Trainium2 kernel written in Bass

```python
import math
import jax, jax.numpy as jnp
from jax import lax
import numpy as np

D_MODEL = 2048
BATCH = 4
SEQ = 4096
DEPTH = 1

SSM_WIDTH = 1024
SSM_GROUP = 16
SSM_GROUPS = SSM_WIDTH // SSM_GROUP
SSM_STATE = 64
CONV_WIDTH = 1024
CONV_GROUP = 64
CONV_K = 3
D_FF = 5632
EPS = 1e-6
DT_MIN = 1e-3
DT_MAX = 1e-1
IN_COLS = SSM_WIDTH + 3 * CONV_WIDTH + 2 * D_MODEL

kernel_name = "hybrid_s5_shortconv_gated_block"


def rmsnorm(x, g):
    xf = x.astype(jnp.float32)
    y = xf * lax.rsqrt(jnp.mean(xf * xf, axis=-1, keepdims=True) + EPS)
    return (y * g.astype(jnp.float32)).astype(x.dtype)


def causal_dwconv(x, w, b):
    c = x.shape[-1]
    y = lax.conv_general_dilated(
        x, w[:, None, :].astype(x.dtype), window_strides=(1,),
        padding=((CONV_K - 1, 0),), dimension_numbers=("NWC", "WIO", "NWC"),
        feature_group_count=c)
    return y + b.astype(x.dtype)


def s5_scan(u, a_re, a_im, log_dt, b_re, b_im, c_re, c_im, d_skip):
    bsz, seq_len, _ = u.shape
    f32 = jnp.float32
    uf = u.astype(f32).reshape(bsz, seq_len, SSM_GROUPS, SSM_GROUP)
    ar = a_re.astype(f32)
    ai = a_im.astype(f32)
    dt = jnp.exp(log_dt.astype(f32))[:, None]
    mag = jnp.exp(dt * ar)
    abar_re = mag * jnp.cos(dt * ai)
    abar_im = mag * jnp.sin(dt * ai)
    nr = abar_re - 1.0
    ni = abar_im
    den = ar * ar + ai * ai
    fr = (nr * ar + ni * ai) / den
    fi = (ni * ar - nr * ai) / den
    br = b_re.astype(f32)
    bi = b_im.astype(f32)
    bbar_re = fr[..., None] * br - fi[..., None] * bi
    bbar_im = fr[..., None] * bi + fi[..., None] * br
    bu_re = jnp.einsum("blgh,gph->blgp", uf, bbar_re)
    bu_im = jnp.einsum("blgh,gph->blgp", uf, bbar_im)
    a_seq_re = jnp.broadcast_to(abar_re[None, None], (1, seq_len, SSM_GROUPS, SSM_STATE))
    a_seq_im = jnp.broadcast_to(abar_im[None, None], (1, seq_len, SSM_GROUPS, SSM_STATE))

    def combine(e1, e2):
        a1r, a1i, b1r, b1i = e1
        a2r, a2i, b2r, b2i = e2
        return (a2r * a1r - a2i * a1i,
                a2r * a1i + a2i * a1r,
                a2r * b1r - a2i * b1i + b2r,
                a2r * b1i + a2i * b1r + b2i)

    _, _, xr, xi = lax.associative_scan(combine, (a_seq_re, a_seq_im, bu_re, bu_im), axis=1)
    y = (jnp.einsum("blgp,ghp->blgh", xr, c_re.astype(f32))
         - jnp.einsum("blgp,ghp->blgh", xi, c_im.astype(f32))
         + d_skip.astype(f32) * uf)
    return y.reshape(bsz, seq_len, SSM_WIDTH).astype(u.dtype)


def token_mixer(xn, w_in, a_re, a_im, log_dt, b_re, b_im, c_re, c_im, d_skip,
                w_glu, w_ssm_out, conv_w, conv_b, w_conv_out, w_o):
    proj = xn @ w_in
    s0 = SSM_WIDTH
    s1 = s0 + CONV_WIDTH
    s2 = s1 + CONV_WIDTH
    s3 = s2 + CONV_WIDTH
    s4 = s3 + D_MODEL
    u = proj[..., :s0]
    v = proj[..., s0:s1]
    gate_b = proj[..., s1:s2]
    gate_c = proj[..., s2:s3]
    merge_a = proj[..., s3:s4]
    merge_b = proj[..., s4:]
    ya = jax.nn.gelu(s5_scan(u, a_re, a_im, log_dt, b_re, b_im, c_re, c_im, d_skip))
    ya = ya * jax.nn.sigmoid(ya @ w_glu)
    ya = ya @ w_ssm_out
    yb = (gate_b * causal_dwconv(gate_c * v, conv_w, conv_b)) @ w_conv_out
    merged = jax.nn.sigmoid(merge_a) * ya + jax.nn.sigmoid(merge_b) * yb
    return merged @ w_o


def conv_ffn(xn, w_up, ffn_conv_w, ffn_conv_b, w_down):
    h = xn @ w_up
    a = causal_dwconv(h[..., :D_FF], ffn_conv_w, ffn_conv_b)
    return (jax.nn.gelu(a) * h[..., D_FF:]) @ w_down


def setup_inputs(seed: int = 0) -> dict:
    key = jax.random.key(seed)
    ks = jax.random.split(key, 24)
    f32 = jnp.float32
    nrm = lambda k, s, sc: jax.random.normal(k, s, f32) * sc
    n_idx = jnp.arange(SSM_STATE, dtype=f32)
    a_re = -0.5 * jnp.exp(nrm(ks[3], (DEPTH, SSM_GROUPS, SSM_STATE), 0.05))
    a_im = math.pi * n_idx[None, None, :] + nrm(ks[4], (DEPTH, SSM_GROUPS, SSM_STATE), 0.05)
    log_dt = jax.random.uniform(ks[5], (DEPTH, SSM_GROUPS), f32, math.log(DT_MIN), math.log(DT_MAX))
    return {
        "x": nrm(ks[0], (BATCH, SEQ, D_MODEL), 1.0),
        "norm_tok": 1.0 + nrm(ks[1], (DEPTH, D_MODEL), 0.01),
        "w_in": nrm(ks[2], (DEPTH, D_MODEL, IN_COLS), D_MODEL ** -0.5),
        "a_re": a_re,
        "a_im": a_im,
        "log_dt": log_dt,
        "b_re": nrm(ks[6], (DEPTH, SSM_GROUPS, SSM_STATE, SSM_GROUP), (2 * SSM_GROUP) ** -0.5),
        "b_im": nrm(ks[7], (DEPTH, SSM_GROUPS, SSM_STATE, SSM_GROUP), (2 * SSM_GROUP) ** -0.5),
        "c_re": nrm(ks[8], (DEPTH, SSM_GROUPS, SSM_GROUP, SSM_STATE), SSM_STATE ** -0.5),
        "c_im": nrm(ks[9], (DEPTH, SSM_GROUPS, SSM_GROUP, SSM_STATE), SSM_STATE ** -0.5),
        "d_skip": nrm(ks[10], (DEPTH, SSM_GROUPS, SSM_GROUP), 1.0),
        "w_glu": nrm(ks[11], (DEPTH, SSM_WIDTH, SSM_WIDTH), SSM_WIDTH ** -0.5),
        "w_ssm_out": nrm(ks[12], (DEPTH, SSM_WIDTH, D_MODEL), SSM_WIDTH ** -0.5),
        "conv_w": nrm(ks[13], (DEPTH, CONV_K, CONV_WIDTH), CONV_K ** -0.5),
        "conv_b": nrm(ks[14], (DEPTH, CONV_WIDTH), 0.01),
        "w_conv_out": nrm(ks[15], (DEPTH, CONV_WIDTH, D_MODEL), CONV_WIDTH ** -0.5),
        "w_o": nrm(ks[16], (DEPTH, D_MODEL, D_MODEL), D_MODEL ** -0.5),
        "norm_ffn": 1.0 + nrm(ks[17], (DEPTH, D_MODEL), 0.01),
        "w_up": nrm(ks[18], (DEPTH, D_MODEL, 2 * D_FF), D_MODEL ** -0.5),
        "ffn_conv_w": nrm(ks[19], (DEPTH, CONV_K, D_FF), CONV_K ** -0.5),
        "ffn_conv_b": nrm(ks[20], (DEPTH, D_FF), 0.01),
        "w_down": nrm(ks[21], (DEPTH, D_FF, D_MODEL), D_FF ** -0.5),
        "norm_final": 1.0 + nrm(ks[22], (D_MODEL,), 0.01),
    }


def reference(x, norm_tok, w_in, a_re, a_im, log_dt, b_re, b_im, c_re, c_im, d_skip,
              w_glu, w_ssm_out, conv_w, conv_b, w_conv_out, w_o,
              norm_ffn, w_up, ffn_conv_w, ffn_conv_b, w_down, norm_final):
    h = x
    for l in range(DEPTH):
        h = h + token_mixer(rmsnorm(h, norm_tok[l]), w_in[l], a_re[l], a_im[l], log_dt[l],
                            b_re[l], b_im[l], c_re[l], c_im[l], d_skip[l],
                            w_glu[l], w_ssm_out[l], conv_w[l], conv_b[l], w_conv_out[l], w_o[l])
        h = h + conv_ffn(rmsnorm(h, norm_ffn[l]), w_up[l], ffn_conv_w[l], ffn_conv_b[l], w_down[l])
    return rmsnorm(h, norm_final)
```

```python
import math
import os
POOL = os.environ.get("K_POOL", "gpsimd")
PF = os.environ.get("K_PF", "1") == "1"
NOACC = os.environ.get("K_NOACC", "0") == "1"
POOL_PSUM = "vector"
import numpy as np
import concourse.bass as bass
import concourse.mybir as mybir
from concourse.bass_utils import run_bass_kernel_spmd

F32 = mybir.dt.float32
BF16 = mybir.dt.bfloat16
AF = mybir.ActivationFunctionType
ALU = mybir.AluOpType

P = 128
D = 2048
KT = 16
DFF = 5632
FT_FF = 44
TOK = 2048
NCOL = 1026
NBLK = 3
BW = 342
TC = 32
NCH = 128
NCX = 65
YW = NCX * TC
EPS = 1e-6
ENGS = ["sync", "gpsimd", "tensor", "scalar", "vector"]
NSLOT = 4
INORDER = set(os.environ.get("K_INORDER", "tensor,vector,scalar").split(","))
SLOTW = 2048

PV_NT, PV_NF, PV_NL = 0, 16, 32
PV_CW, PV_CB = 48, 72
PV_FW, PV_FB = 80, 212
PV_DS = 256
PV_SGN, PV_NSGN, PV_EV, PV_OD = 264, 265, 266, 267
NPV = 268


class Op:
    __slots__ = ("eng", "fn", "deps", "dma_sem", "flag", "count", "pos")

    def __init__(self, eng, fn, deps, dma_sem=None):
        self.eng = eng
        self.fn = fn
        self.deps = deps
        self.dma_sem = dma_sem
        self.flag = False
        self.count = None
        self.pos = 0


class Buf:
    __slots__ = ("w", "r")

    def __init__(self):
        self.w = []
        self.r = {}

    def rdeps(self):
        return list(self.w)

    def wdeps(self):
        return list(self.w) + list(self.r.values())


class Sched:
    def __init__(self):
        self.q = {e: [] for e in ENGS}
        self.dma_counts = {}
        self.last_dma = {}
        self.npos = 0

    def op(self, eng, fn, reads=(), writes=(), extra=(), dma=None, serial=True):
        deps = []
        if dma is not None and serial and dma in self.last_dma:
            deps.append(self.last_dma[dma])
        for b in reads:
            deps += b.rdeps()
        for b in writes:
            deps += b.wdeps()
        deps += [d for d in extra if d is not None]
        o = Op(eng, fn, deps, dma)
        self.npos += 1
        o.pos = self.npos
        if dma is not None:
            self.dma_counts[dma] = self.dma_counts.get(dma, 0) + 16
            o.count = self.dma_counts[dma]
            self.last_dma[dma] = o
        self.q[eng].append(o)
        for b in reads:
            key = eng if dma is None else ("dma", dma, o.pos)
            b.r[key] = o
        for b in writes:
            b.w = [o]
            b.r = {}
        return o

    def part(self, eng, fn, reads, buf, first, extra=()):
        ex = list(extra) + (buf.wdeps() if first else list(buf.w))
        o = self.op(eng, fn, reads=reads, extra=ex)
        if first:
            buf.w = [o]
            buf.r = {}
        else:
            buf.w.append(o)
        return o

    def finalize(self):
        for e in ENGS:
            for o in self.q[e]:
                for d in o.deps:
                    if d.dma_sem is None:
                        if d.eng == o.eng and d.eng in INORDER:
                            continue
                        d.flag = True
        for e in ENGS:
            c = 0
            for o in self.q[e]:
                if o.dma_sem is None and o.flag:
                    c += 1
                    o.count = c

    def emit(self, eng_name, e, sems):
        waited = {}
        for o in self.q[eng_name]:
            need = {}
            for d in o.deps:
                if d.dma_sem is not None:
                    key = "D_" + d.dma_sem
                else:
                    if d.eng == eng_name and eng_name in INORDER:
                        continue
                    key = "E_" + d.eng
                if need.get(key, 0) < d.count:
                    need[key] = d.count
            for key, cnt in need.items():
                if waited.get(key, 0) < cnt:
                    e.wait_ge(sems[key], cnt)
                    waited[key] = cnt
            ins = o.fn(e)
            if o.dma_sem is not None:
                ins.then_inc(sems["D_" + o.dma_sem], 16)
            elif o.flag:
                ins.then_inc(sems["E_" + eng_name], 1)


class PsumRing:
    def __init__(self, ps):
        self.aps = [ps[:, 512 * i:512 * (i + 1)] for i in range(8)]
        self.bufs = [Buf() for _ in range(8)]
        self.held = set()
        self.i = 0

    def get(self):
        while True:
            k = self.i % 8
            self.i += 1
            if k not in self.held:
                return self.aps[k], self.bufs[k]

    def hold(self, n):
        out = []
        while len(out) < n:
            k = self.i % 8
            self.i += 1
            if k not in self.held:
                self.held.add(k)
                out.append((self.aps[k], self.bufs[k], k))
        return out

    def hold_specific(self, ks):
        out = []
        for k in ks:
            self.held.add(k)
            out.append((self.aps[k], self.bufs[k], k))
        return out

    def unhold(self, banks):
        for b in banks:
            self.held.discard(b[2])


class WeightStream:
    def __init__(self, S, slots, plan=None):
        self.S = S
        self.slots = slots
        self.bufs = [Buf() for _ in slots]
        self.plan = plan
        self.log = []
        self.issued = 0
        self.k = 0
        self.released = set()
        self.nrel = 0

    def _issue(self, m):
        w, r0, nk, c0, ncol = self.plan[m]
        s = m % NSLOT
        dst = self.slots[s][:, 0:nk * ncol].rearrange("p (k c) -> p k c", k=nk)
        src = w[r0:r0 + nk * P, c0:c0 + ncol].rearrange("(k p) c -> p k c", p=P)
        self.S.op("gpsimd", lambda e, dst=dst, src=src: e.dma_start(out=dst, in_=src),
                  writes=[self.bufs[s]], dma="w%d" % s)

    def _pump(self):
        if self.plan is None:
            return
        while self.issued < len(self.plan) and self.issued < self.nrel + NSLOT:
            self._issue(self.issued)
            self.issued += 1

    def release(self, k):
        self.released.add(k)
        while self.nrel in self.released:
            self.nrel += 1
        self._pump()

    def get(self, w, r0, nk, c0, ncol):
        assert nk * ncol <= 2 * SLOTW
        spec = (w, r0, nk, c0, ncol)
        k = self.k
        self.k += 1
        s = k % NSLOT
        view = self.slots[s][:, 0:nk * ncol].rearrange("p (k c) -> p k c", k=nk)
        if self.plan is None:
            self.log.append(spec)
            return view, self.bufs[s], k
        assert self.plan[k][1:] == spec[1:]
        self._pump()
        assert k < self.issued, "too many weight chunks held at once"
        return view, self.bufs[s], k


def build(debug=False, limit=None):
    nc = bass.Bass("TRN2", target_bir_lowering=False)
    dt_ = nc.dram_tensor
    xT = dt_("xT", [D, 2 * TOK], F32, kind="ExternalInput").ap()
    w_in = dt_("w_in", [D, 8192], F32, kind="ExternalInput").ap()
    w_glu = dt_("w_glu", [1024, 1024], F32, kind="ExternalInput").ap()
    w_so = dt_("w_ssm_out", [1024, D], F32, kind="ExternalInput").ap()
    w_co = dt_("w_conv_out", [1024, D], F32, kind="ExternalInput").ap()
    w_o = dt_("w_o", [D, D], F32, kind="ExternalInput").ap()
    w_up = dt_("w_up", [D, 2 * DFF], F32, kind="ExternalInput").ap()
    w_dn = dt_("w_down", [DFF, D], F32, kind="ExternalInput").ap()
    pvec_d = dt_("pvec", [P, NPV], F32, kind="ExternalInput").ap()
    s5a_d = dt_("s5a", [P, 3 * 64], F32, kind="ExternalInput").ap()
    s5b_d = dt_("s5b", [P, 2 * 1024], F32, kind="ExternalInput").ap()
    s5c_d = dt_("s5c", [P, 2 * 1024], F32, kind="ExternalInput").ap()
    cst_d = dt_("cst", [P, 5 * 128], F32, kind="ExternalInput").ap()
    outT = dt_("outT", [D, TOK], F32, kind="ExternalOutput").ap()
    ya0_d = dt_("ya0_d", [P, 8 * YW], BF16, kind=("ExternalOutput" if debug else "Internal")).ap()

    NW = 48256
    dbg_t = {}
    if debug:
        for nm, n, dty in [("d_xn", 16 * NCOL, BF16), ("d_yb", 8 * NCOL, BF16), ("d_ya", 8 * NCOL, BF16),
                           ("d_mg", 16 * NCOL, BF16), ("d_h1", 16 * NCOL, F32), ("d_hn", 16 * NCOL, BF16),
                           ("d_h2", 16 * NCOL, F32), ("d_ba", NCOL, F32), ("d_bb", NCOL, F32), ("d_sa", NCOL, F32)]:
            dbg_t[nm] = dt_(nm, [P, n], dty, kind="ExternalOutput").ap()
    sem_names = ["E_" + e for e in ENGS] + ["D_w%d" % i for i in range(NSLOT)] + \
        ["D_cst", "D_xs0", "D_xs1", "D_ya", "D_out", "D_wu", "D_yaL", "D_dbg", "D_xq0", "D_xq1", "D_xq2", "D_xq3"]

    import contextlib
    with contextlib.ExitStack() as es:
        AR = es.enter_context(nc.sbuf_tensor("arena", [P, NW], F32))
        PSUM = es.enter_context(nc.psum_tensor("ps", [P, 4096], F32))
        sems = {n: es.enter_context(nc.semaphore(n)) for n in sem_names}
        block = es.enter_context(nc.Block())

        def fv(off, n):
            return AR[:, off:off + n]

        def bv(off, nbf):
            return AR[:, off:off + nbf // 2].bitcast(BF16)

        o = 0
        PV = fv(o, NPV); o += NPV
        ONES = bv(o, 128); o += 64
        CVC = fv(o, 16); o += 16
        HAC = fv(o, 88); o += 88
        RSTD = fv(o, NCOL); o += NCOL
        IDENT = fv(o, 128); o += 128
        RSA = fv(o, 2 * NCOL); o += 2 * NCOL
        o_persist = o
        RING0 = 3648
        assert o_persist <= RING0
        slots = [bv(RING0 + i * SLOTW, 2 * SLOTW) for i in range(NSLOT)]
        BASE = RING0 + NSLOT * SLOTW

        def pcol(c):
            return PV[:, c:c + 1]

        def program(S, W):
            PS = PsumRing(PSUM)

            def dump(nm, ap, bufs):
                if debug:
                    S.op("sync", lambda e: e.dma_start(out=dbg_t[nm], in_=ap), reads=bufs, dma="dbg")

            bPV, bONES, bCST = Buf(), Buf(), Buf()
            S.op("sync", lambda e: e.dma_start(out=PV, in_=pvec_d), writes=[bPV], dma="cst")
            S.op("sync", lambda e: e.dma_start(out=IDENT, in_=cst_d[:, 0:128]), writes=[bCST], dma="cst")
            S.op("vector", lambda e: e.memset(ONES, 1.0), writes=[bONES])
            bCVC, bHAC = Buf(), Buf()
            S.op("vector", lambda e: e.memset(CVC, 0.0), writes=[bCVC])
            S.op("vector", lambda e: e.memset(HAC, 0.0), writes=[bHAC])

            o = RING0
            UALL = bv(o, 8 * 4096); o += 16384
            UALLr = UALL.rearrange("p (f s c) -> p f s c", f=8, s=TC)
            o_u = o
            WU = bv(o, 16 * 1024); o += 8192
            WUv = WU.rearrange("p (k c) -> p k c", k=16)
            XS = [fv(o + i * 4096, 4096) for i in range(2)]; o += 8192
            SQB = bv(o, 4096); o += 2048
            XNB = [bv(o + i * 2048, 4096) for i in range(2)]; o += 4096
            SD = fv(o, 256); o += 256
            RS = fv(o, 256); o += 256
            assert o <= NW, o
            bWUq = [Buf() for _ in range(4)]
            for i in range(4):
                S.op("gpsimd", lambda e, i=i: e.dma_start(
                    out=WUv[:, :, i * 256:(i + 1) * 256],
                    in_=w_in[:, i * 256:(i + 1) * 256].rearrange("(k p) c -> p k c", p=P)),
                    writes=[bWUq[i]], dma="wu")
            bWU = Buf()
            bWU.w = [S.q["gpsimd"][-1]]
            xTv = xT.rearrange("(k p) n -> p k n", p=P)
            bXS = [Buf(), Buf()]
            bSQB, bSD, bRS = Buf(), Buf(), Buf()
            bXNB = [Buf(), Buf()]
            bU = [Buf() for _ in range(8)]
            UB = 256
            bRSA = Buf()
            NB_U = 2 * TOK // UB

            def u1_front_a(blk):
                sl = blk % 2
                c0 = blk * UB
                xs3 = XS[sl].rearrange("p (k n) -> p k n", k=16)
                S.op("sync", lambda e, xs3=xs3, c0=c0: e.dma_start(out=xs3, in_=xTv[:, :, c0:c0 + UB]),
                     writes=[bXS[sl]], dma="xs%d" % sl)
                S.op("scalar", lambda e, sl=sl: e.activation(out=SQB, in_=XS[sl], func=AF.Square),
                     reads=[bXS[sl]], writes=[bSQB])

            def u1_front_b(blk):
                sl = blk % 2
                c0 = blk * UB
                xs3 = XS[sl].rearrange("p (k n) -> p k n", k=16)
                pap, pb = PS.get()
                sq3 = SQB.rearrange("p (k n) -> p k n", k=16)

                def f_stat(e, pap=pap, sq3=sq3):
                    last = None
                    for k in range(16):
                        last = e.matmul(pap[:, 0:UB], lhsT=ONES, rhs=sq3[:, k, :], start=(k == 0), stop=(k == 15))
                    return last
                S.op("tensor", f_stat, reads=[bSQB, bONES], writes=[pb])
                S.op("scalar", lambda e, pap=pap: e.activation(out=SD, in_=pap[:, 0:UB], func=AF.Sqrt,
                                                                bias=EPS, scale=1.0 / D),
                     reads=[pb], writes=[bSD])
                S.op("vector", lambda e: e.reciprocal(out=RS, in_=SD), reads=[bSD], writes=[bRS])
                t0 = max(c0, 2044)
                if t0 < c0 + UB:
                    S.part(POOL, lambda e, t0=t0, c0=c0: e.tensor_copy(out=RSA[:, t0 - 2044:c0 + UB - 2044], in_=RS[:, t0 - c0:UB]),
                           [bRS], bRSA, t0 == 2044)
                xn3 = XNB[sl].rearrange("p (k n) -> p k n", k=16)

                def f_xn(e, xs3=xs3, xn3=xn3):
                    last = None
                    for k in range(16):
                        last = e.scalar_tensor_tensor(out=xn3[:, k, :], in0=xs3[:, k, :], scalar=pcol(PV_NT + k),
                                                      in1=RS, op0=ALU.mult, op1=ALU.mult)
                    return last
                S.op("vector", f_xn, reads=[bXS[sl], bRS, bPV], writes=[bXNB[sl]])

            def u1_back(blk, fts):
                sl = blk % 2
                c0 = blk * UB
                xn3 = XNB[sl].rearrange("p (k n) -> p k n", k=16)
                for ft in fts:
                    pap, pb = PS.get()

                    def f_u(e, pap=pap, ft=ft, xn3=xn3):
                        last = None
                        for k in range(16):
                            last = e.matmul(pap[:, 0:UB], lhsT=WUv[:, k, ft * 128:(ft + 1) * 128], rhs=xn3[:, k, :],
                                            start=(k == 0), stop=(k == 15))
                        return last
                    S.op("tensor", f_u, reads=[bXNB[sl], bWUq[ft // 2]], writes=[pb])
                    cb = c0 // TC
                    ncb = UB // TC
                    src_ = pap[:, 0:UB].rearrange("p (c s) -> p s c", s=TC)
                    if ft % 2 == 0:
                        S.op("scalar", lambda e, src_=src_, ft=ft, cb=cb, ncb=ncb: e.activation(
                            out=UALLr[:, ft, :, cb:cb + ncb], in_=src_, func=AF.Copy),
                            reads=[pb], writes=[bU[ft]])
                    else:
                        S.op("vector", lambda e, src_=src_, ft=ft, cb=cb, ncb=ncb: e.tensor_copy(
                            out=UALLr[:, ft, :, cb:cb + ncb], in_=src_),
                            reads=[pb], writes=[bU[ft]])
            u1_front_a(0)
            u1_front_b(0)
            for blk in range(NB_U):
                if blk + 1 < NB_U:
                    u1_front_a(blk + 1)
                u1_back(blk, range(0, 4))
                if blk + 1 < NB_U:
                    u1_front_b(blk + 1)
                u1_back(blk, range(4, 8))
            bU1_done = [bXS[0], bXS[1], bSQB, bXNB[0], bXNB[1], bWU, bSD, bRS]
            if limit == "U1":
                return

            o = o_u
            SALL = fv(o, 64 * 128); o += 8192
            SALLv = SALL.rearrange("p (g c) -> p g c", g=64)
            SM = [fv(o + i * 64, 64) for i in range(26)]; o += 26 * 64
            (tAR, tAI, tLDT, tDT, tDAR, tDAI, tMAG, tS16, tC16, tZR, tZI, t1, t2, t3, tNR, tDEN, tFR, tFI, tF2,
             tLR, tLI, tA2r, tNA2r, tx1, tx2, tx3) = SM
            BST = fv(o, 1024); o += 1024
            PWR = fv(o, 33 * 64); o += 2112
            PWI = fv(o, 33 * 64); o += 2112
            PWRv = PWR.rearrange("p (s g) -> p s g", s=33)
            PWIv = PWI.rearrange("p (s g) -> p s g", s=33)
            XBF = bv(o, 64 * NCX + 64); o += (64 * NCX + 64) // 2
            XBFv = XBF[:, 0:64 * NCX].rearrange("p (g c) -> p g c", g=64)
            o_dead = o
            b1o = o
            B1 = fv(o, 1024); o += 1024
            B2 = fv(o, 1024); o += 1024
            BSW = fv(o, 1024); o += 1024
            EST = fv(o, 2048); o += 2048
            TMPE = fv(o, 2048); o += 2048
            ETE = bv(b1o, 4096)
            ETO = bv(o, 4096); o += 2048
            ZA = fv(o, 128); o += 128
            ZB = fv(o, 128); o += 128
            ZT1 = fv(o, 128); o += 128
            ZT2 = fv(o, 128); o += 128
            A1W = fv(o, 128); o += 128
            A2W = fv(o, 128); o += 128
            PERM = fv(o, 128); o += 128
            o_sweepA_end = o
            assert o <= NW, o
            ch = Buf()
            for b in bU1_done:
                ch.w += b.wdeps()
            ch.w = list(ch.w)
            S.op("sync", lambda e: e.dma_start(out=fv(o_u + 8192, 192), in_=s5a_d), writes=[ch], dma="cst")
            S.op("sync", lambda e: e.dma_start(out=fv(b1o, 2048), in_=s5b_d), writes=[ch], dma="cst")
            S.op("sync", lambda e: e.dma_start(out=PERM, in_=cst_d[:, 128:256]), writes=[ch], dma="cst")

            def V(fn):
                return S.op("vector", fn, writes=[ch])

            def A(fn):
                return S.op("scalar", fn, writes=[ch])

            def vtt(out, a, b, op):
                return V(lambda e: e.tensor_tensor(out=out, in0=a, in1=b, op=op))

            A(lambda e: e.activation(out=tDT, in_=tLDT, func=AF.Exp))
            vtt(tDAR, tDT, tAR, ALU.mult)
            vtt(tDAI, tDT, tAI, ALU.mult)
            A(lambda e: e.activation(out=tMAG, in_=tDAR, func=AF.Exp, scale=1.0 / 16))
            A(lambda e: e.activation(out=tS16, in_=tDAI, func=AF.Sin, scale=1.0 / 16))
            V(lambda e: e.tensor_scalar(tx1, tDAI, 1.0 / 16, math.pi / 2, ALU.mult, ALU.add))
            A(lambda e: e.activation(out=tC16, in_=tx1, func=AF.Sin))
            vtt(tZR, tMAG, tC16, ALU.mult)
            vtt(tZI, tMAG, tS16, ALU.mult)

            def csq(zr, zi):
                vtt(t1, zr, zr, ALU.mult)
                vtt(t2, zi, zi, ALU.mult)
                vtt(t3, zr, zi, ALU.mult)
                vtt(zr, t1, t2, ALU.subtract)
                V(lambda e: e.tensor_scalar(zi, t3, 2.0, None, ALU.mult))
            for _ in range(4):
                csq(tZR, tZI)
            V(lambda e: e.tensor_scalar(tNR, tZR, -1.0, None, ALU.add))
            vtt(t1, tAR, tAR, ALU.mult)
            vtt(t2, tAI, tAI, ALU.mult)
            vtt(tDEN, t1, t2, ALU.add)
            V(lambda e: e.reciprocal(out=tDEN, in_=tDEN))
            vtt(t1, tNR, tAR, ALU.mult)
            vtt(t2, tZI, tAI, ALU.mult)
            vtt(t1, t1, t2, ALU.add)
            vtt(tFR, t1, tDEN, ALU.mult)
            vtt(t1, tZI, tAR, ALU.mult)
            vtt(t2, tNR, tAI, ALU.mult)
            vtt(t1, t1, t2, ALU.subtract)
            vtt(tFI, t1, tDEN, ALU.mult)
            V(lambda e: e.tensor_scalar(tF2, tFI, pcol(PV_SGN), None, ALU.mult))
            B1v = B1.rearrange("p (g h) -> p g h", g=64)
            B2v = B2.rearrange("p (g h) -> p g h", g=64)
            BSTv = BST.rearrange("p (g h) -> p g h", g=64)
            BSWv = BSW.rearrange("p (g h) -> p g h", g=64)

            def bc_h(t):
                return t.unsqueeze(2).broadcast_to([P, 64, 16])
            vtt(BSTv, B1v, bc_h(tFR), ALU.mult)
            vtt(BSWv, B2v, bc_h(tF2), ALU.mult)
            vtt(BSTv, BSTv, BSWv, ALU.add)
            vtt(BSWv, B2v, bc_h(tFR), ALU.mult)
            vtt(B2v, B1v, bc_h(tF2), ALU.mult)
            vtt(BSWv, BSWv, B2v, ALU.subtract)
            V(lambda e: e.tensor_scalar(BSW, BSW, pcol(PV_SGN), None, ALU.mult))
            V(lambda e: e.memset(PWRv[:, 0, :], 1.0))
            V(lambda e: e.memset(PWIv[:, 0, :], 0.0))
            V(lambda e: e.tensor_copy(out=tLR, in_=tZR))
            V(lambda e: e.tensor_copy(out=tLI, in_=tZI))
            TA = TMPE[:, 0:1024]
            TB = TMPE[:, 1024:2048]
            n = 1
            while n <= 16:
                def bcs(t, n=n):
                    return t.unsqueeze(1).broadcast_to([P, n, 64])
                ta = TA.rearrange("p (s g) -> p s g", g=64)[:, 0:n, :]
                tb = TB.rearrange("p (s g) -> p s g", g=64)[:, 0:n, :]
                vtt(ta, PWRv[:, 0:n, :], bcs(tLR), ALU.mult)
                vtt(tb, PWIv[:, 0:n, :], bcs(tLI), ALU.mult)
                vtt(PWRv[:, n:2 * n, :], ta, tb, ALU.subtract)
                vtt(ta, PWRv[:, 0:n, :], bcs(tLI), ALU.mult)
                vtt(tb, PWIv[:, 0:n, :], bcs(tLR), ALU.mult)
                vtt(PWIv[:, n:2 * n, :], ta, tb, ALU.add)
                csq(tLR, tLI)
                n *= 2
            V(lambda e: e.tensor_copy(out=PWRv[:, 32, :], in_=tLR))
            V(lambda e: e.tensor_copy(out=PWIv[:, 32, :], in_=tLI))
            V(lambda e: e.tensor_copy(out=A1W[:, 0:64], in_=tLR))
            V(lambda e: e.tensor_copy(out=A1W[:, 64:128], in_=tLR))
            V(lambda e: e.tensor_scalar(A2W[:, 0:64], tLI, pcol(PV_SGN), None, ALU.mult))
            V(lambda e: e.tensor_scalar(A2W[:, 64:128], tLI, pcol(PV_NSGN), None, ALU.mult))
            bPW = ch

            if limit == "DERIVE":
                return
            bEST, bTMPE = Buf(), Buf()
            bETE = [Buf(), Buf()]
            bETO = [Buf(), Buf()]
            for b in [bEST, bTMPE] + bETE + bETO:
                b.w = list(ch.w)
            ESTv = EST.rearrange("p (q c) -> p q c", q=16)
            ETEv = ETE.rearrange("p (q c) -> p q c", q=32)
            ETOv = ETO.rearrange("p (q c) -> p q c", q=32)
            bSALL = Buf()
            sbanks = PS.hold(4)
            sbuf2 = []
            for _ in range(4):
                b_ = Buf()
                sbuf2.append([b_, b_])

            def salle_evac(ft):
                hh = ft % 2
                for j in range(4):
                    g0 = ft * 8 + 2 * j
                    dst = SALL[:, g0 * 128:(g0 + 2) * 128]
                    srcp = sbanks[j][0][:, hh * 256:(hh + 1) * 256]
                    firstw = (ft == 0 and j == 0)
                    if j % 2 == 0:
                        S.part("scalar", lambda e, dst=dst, srcp=srcp: e.activation(out=dst, in_=srcp, func=AF.Copy),
                               [sbuf2[j][hh]], bSALL, firstw, extra=list(ch.w) if firstw else [])
                    else:
                        S.part("vector", lambda e, dst=dst, srcp=srcp: e.tensor_copy(out=dst, in_=srcp),
                               [sbuf2[j][hh]], bSALL, firstw)
            pending = None
            SB_PERSIST = os.environ.get("K_SB", "persist") == "persist"
            for ft in range(8):
                hh = ft % 2
                for hf in range(2):
                    q0 = hf * 16
                    pw_r = PWRv[:, q0:q0 + 16, ft * 8:(ft + 1) * 8].unsqueeze(3).broadcast_to([P, 16, 8, 16])
                    pw_i = PWIv[:, q0:q0 + 16, ft * 8:(ft + 1) * 8].unsqueeze(3).broadcast_to([P, 16, 8, 16])
                    bst = BST[:, ft * 128:(ft + 1) * 128].rearrange("p (g h) -> p g h", g=8).unsqueeze(1).broadcast_to([P, 16, 8, 16])
                    bsw = BSW[:, ft * 128:(ft + 1) * 128].rearrange("p (g h) -> p g h", g=8).unsqueeze(1).broadcast_to([P, 16, 8, 16])
                    e4 = EST.rearrange("p (q g h) -> p q g h", q=16, g=8)
                    t4 = TMPE.rearrange("p (q g h) -> p q g h", q=16, g=8)
                    S.op("vector", lambda e, e4=e4, pw_r=pw_r, bst=bst: e.tensor_tensor(out=e4, in0=pw_r, in1=bst, op=ALU.mult),
                         reads=[bPW], writes=[bEST])
                    S.op(POOL, lambda e, t4=t4, pw_i=pw_i, bsw=bsw: e.tensor_tensor(out=t4, in0=pw_i, in1=bsw, op=ALU.mult),
                         reads=[bPW], writes=[bTMPE])
                    S.op("vector", lambda e: e.tensor_tensor(out=EST, in0=EST, in1=TMPE, op=ALU.add),
                         reads=[bTMPE], writes=[bEST])
                    for qq in range(4):
                        pap, pb = PS.get()

                        def f_tr(e, pap=pap, qq=qq):
                            last = None
                            for i in range(4):
                                last = e.transpose(out=pap[:, i * 128:(i + 1) * 128], in_=ESTv[:, qq * 4 + i, :], identity=IDENT)
                            return last
                        S.op("tensor", f_tr, reads=[bEST, bCST], writes=[pb])
                        qa = q0 + qq * 4
                        ete_op = S.part("vector", lambda e, pap=pap, qa=qa: e.tensor_scalar(
                            ETE[:, qa * 128:(qa + 4) * 128], pap[:, 0:512], pcol(PV_EV), None, ALU.mult),
                            [pb, bPV], bETE[hf], qq == 0)
                        S.part("scalar", lambda e, pap=pap, qa=qa: e.activation(
                            out=ETO[:, qa * 128:(qa + 4) * 128], in_=pap[:, 0:512], func=AF.Copy, scale=pcol(PV_OD)),
                            [pb, bPV], bETO[hf], qq == 0, extra=[ete_op])
                    if hf == 0 and pending is not None and SB_PERSIST:
                        salle_evac(pending)
                        pending = None
                    uv = UALLr[:, ft, :, :]

                    def f_sc(e, q0=q0, uv=uv, hf=hf, hh=hh):
                        last = None
                        for q in range(q0, q0 + 16):
                            s = 31 - q
                            for j in range(4):
                                for par in range(2):
                                    et = ETEv if par == 0 else ETOv
                                    cc = hh * 256 + par * 128
                                    last = e.matmul(sbanks[j][0][:, cc:cc + 128],
                                                    lhsT=et[32 * j:32 * j + 32, q, :],
                                                    rhs=uv[32 * j:32 * j + 32, s, :],
                                                    start=(hf == 0 and q == q0 and par == 0),
                                                    stop=(q == 31 and par == 1),
                                                    skip_group_check=True, tile_position=(32 * j, 0))
                        return last
                    if hf == 0:
                        S.op("tensor", f_sc, reads=[bETE[hf], bETO[hf], bU[ft]], writes=[sbuf2[j][hh] for j in range(4)])
                    else:
                        o_ = S.op("tensor", f_sc, reads=[bETE[hf], bETO[hf], bU[ft]])
                        for j in range(4):
                            sbuf2[j][hh].w = [o_]
                pending = ft
                if not SB_PERSIST:
                    salle_evac(pending)
                    pending = None
            if pending is not None:
                salle_evac(pending)
            PS.unhold(sbanks)
            if limit == "A":
                return
            bZa, bZb = Buf(), Buf()
            bZb.w = list(ch.w)
            S.op("vector", lambda e: e.memset(ZA, 0.0), writes=[bZa], reads=[ch])
            bXBF = Buf()
            cur, nxt = ZA, ZB
            bcur, bnxt = bZa, bZb
            for cb in range(NCH // 8):
                pap, pb = PS.get()
                S.op("tensor", lambda e, pap=pap, cb=cb: e.matmul(
                    pap[:, 0:512].rearrange("p (g c) -> p g c", g=64), lhsT=PERM, rhs=SALLv[:, :, cb * 8:(cb + 1) * 8],
                    start=True, stop=True), reads=[bSALL, ch], writes=[pb])
                ssw = pap[:, 0:512].rearrange("p (g c) -> p g c", g=64)
                for ci in range(8):
                    c = cb * 8 + ci
                    if c >= 63:
                        S.part("scalar", lambda e, cur=cur, c=c: e.activation(out=XBFv[:, :, c - 63], in_=cur[:, 0:64], func=AF.Copy),
                               [bcur], bXBF, c == 63)
                    if c == NCH - 1:
                        break

                    def f_step(e, cur=cur, nxt=nxt, c=c, ci=ci, ssw=ssw):
                        e.tensor_tensor(out=ZT1, in0=cur, in1=A1W, op=ALU.mult)
                        e.tensor_tensor(out=ZT2[:, 0:64], in0=cur[:, 64:128], in1=A2W[:, 0:64], op=ALU.mult)
                        e.tensor_tensor(out=ZT2[:, 64:128], in0=cur[:, 0:64], in1=A2W[:, 64:128], op=ALU.mult)
                        e.tensor_tensor(out=ZT1, in0=ZT1, in1=ZT2, op=ALU.add)
                        e.tensor_tensor(out=nxt[:, 0:64], in0=ZT1[:, 0:64], in1=SALLv[:, :, c], op=ALU.add)
                        return e.tensor_tensor(out=nxt[:, 64:128], in0=ZT1[:, 64:128], in1=ssw[:, :, ci], op=ALU.add)
                    S.op("vector", f_step, reads=[pb, bSALL, bcur], writes=[bnxt])
                    cur, nxt = nxt, cur
                    bcur, bnxt = bnxt, bcur

            if limit == "REC":
                return
            o = o_dead
            c1o = o
            C1 = fv(o, 1024); o += 1024
            C2 = fv(o, 1024); o += 1024
            FT32 = fv(o, 33 * 128); o += 4224
            KB = bv(o, 32 * 128); o += 2048
            YST = [bv(o + i * (YW // 2), YW) for i in range(2)]; o += YW
            DD = fv(o, 128); o += 128
            bdmo = o
            BDM = fv(o, 128); o += 128
            ECOL = fv(o, 128); o += 128
            OCOL = fv(o, 128); o += 128
            assert o <= NW, o
            o2 = o_u
            FTT = fv(o2, 4224); o2 += 4224
            FTE = bv(o2, 4096); o2 += 2048
            FTO = bv(o2, 4096); o2 += 2048
            assert o2 <= o_u + 8192 + 26 * 64
            bC = Buf()
            bC.w = bZa.wdeps() + bZb.wdeps() + bXBF.wdeps() + bSALL.wdeps()
            S.op("sync", lambda e: e.dma_start(out=fv(c1o, 2048), in_=s5c_d), writes=[bC], dma="cst")
            S.op("sync", lambda e: e.dma_start(out=fv(bdmo, 384), in_=cst_d[:, 256:640]), writes=[bC], dma="cst")
            S.op("vector", lambda e: e.tensor_scalar(C1, C1, pcol(PV_NSGN), None, ALU.mult), reads=[bPV], writes=[bC])
            S.op("vector", lambda e: e.tensor_scalar(C2, C2, -1.0, None, ALU.mult), writes=[bC])
            bFT32, bFTT, bFTE, bFTO, bKB, bDD = Buf(), Buf(), Buf(), Buf(), Buf(), Buf()
            bFTT.w = list(bC.w)
            bFTE.w = list(bC.w)
            bFTO.w = list(bC.w)
            S.op(POOL, lambda e: e.memset(FTE, 0.0), writes=[bFTE])
            S.op(POOL, lambda e: e.memset(FTO, 0.0), writes=[bFTO])
            bYST = [Buf(), Buf()]
            FT32v = FT32.rearrange("p (s c) -> p s c", s=33)
            FTEv = FTE.rearrange("p (r c) -> p r c", r=32)
            FTOv = FTO.rearrange("p (r c) -> p r c", r=32)
            KBv = KB.rearrange("p (t c) -> p t c", t=32)
            rr = [(0, 7), (7, 14), (14, 21), (21, 28), (28, 32)]
            for ft in range(8):
                pw_r = PWRv[:, :, ft * 8:(ft + 1) * 8].unsqueeze(3).broadcast_to([P, 33, 8, 16])
                pw_i = PWIv[:, :, ft * 8:(ft + 1) * 8].unsqueeze(3).broadcast_to([P, 33, 8, 16])
                a1 = C1[:, ft * 128:(ft + 1) * 128].rearrange("p (g h) -> p g h", g=8).unsqueeze(1).broadcast_to([P, 33, 8, 16])
                a2 = C2[:, ft * 128:(ft + 1) * 128].rearrange("p (g h) -> p g h", g=8).unsqueeze(1).broadcast_to([P, 33, 8, 16])
                f4 = FT32.rearrange("p (s g h) -> p s g h", s=33, g=8)
                t4 = FTT.rearrange("p (s g h) -> p s g h", s=33, g=8)
                S.op("vector", lambda e, f4=f4, a1=a1, pw_r=pw_r: e.tensor_tensor(out=f4, in0=a1, in1=pw_r, op=ALU.mult),
                     reads=[bC, bPW], writes=[bFT32])
                S.op(POOL, lambda e, t4=t4, a2=a2, pw_i=pw_i: e.tensor_tensor(out=t4, in0=a2, in1=pw_i, op=ALU.mult),
                     reads=[bC, bPW], writes=[bFTT])
                S.op("vector", lambda e: e.tensor_tensor(out=FT32, in0=FT32, in1=FTT, op=ALU.add),
                     reads=[bFTT], writes=[bFT32])
                S.op("vector", lambda e, ft=ft: e.tensor_scalar(DD, IDENT, pcol(PV_DS + ft), None, ALU.mult),
                     reads=[bCST, bPV], writes=[bDD])
                for s0 in range(0, 32, 4):
                    pap, pb = PS.get()

                    def f_k(e, pap=pap, s0=s0, ft=ft):
                        last = e.matmul(pap[:, 0:512], lhsT=BST[:, ft * 128:(ft + 1) * 128],
                                        rhs=FT32[:, s0 * 128:(s0 + 4) * 128], start=True, stop=(s0 != 0))
                        if s0 == 0:
                            last = e.matmul(pap[:, 0:128], lhsT=IDENT, rhs=DD, start=False, stop=True, skip_group_check=True)
                        return last
                    S.op("tensor", f_k, reads=[bFT32, bPW, bDD, bCST], writes=[pb])
                    bdb = BDM.unsqueeze(1).broadcast_to([P, 4, 128])
                    S.part("vector", lambda e, pap=pap, s0=s0, bdb=bdb: e.tensor_tensor(
                        out=KBv[:, s0:s0 + 4, :], in0=pap[:, 0:512].rearrange("p (t c) -> p t c", t=4), in1=bdb, op=ALU.mult),
                        [pb, bC], bKB, s0 == 0)
                ecb = ECOL.unsqueeze(1).broadcast_to([P, 32, 128])
                ocb = OCOL.unsqueeze(1).broadcast_to([P, 32, 128])
                f5 = FT32v[:, 1:33, :].rearrange("p r (g two h) -> p r g two h", two=2, h=16)
                S.op("scalar", lambda e, f5=f5: e.activation(
                    out=FTEv.rearrange("p r (g two h) -> p r g two h", two=2, h=16)[:, :, :, 0, :], in_=f5[:, :, :, 0, :], func=AF.Copy),
                    reads=[bFT32], writes=[bFTE])
                S.op("scalar", lambda e, f5=f5: e.activation(
                    out=FTOv.rearrange("p r (g two h) -> p r g two h", two=2, h=16)[:, :, :, 1, :], in_=f5[:, :, :, 1, :], func=AF.Copy),
                    reads=[bFT32], writes=[bFTO])
                uv = UALLr[:, ft, :, :]
                ysl = ft % 2
                ystv = YST[ysl].rearrange("p (c r) -> p c r", r=TC)
                for (ra, rb) in rr:
                    pap, pb = PS.get()

                    def f_y(e, pap=pap, ra=ra, rb=rb, uv=uv, ft=ft):
                        last = None
                        for tau in range(rb):
                            r0 = max(ra, tau)
                            nr = rb - r0
                            outv = pap[:, (r0 - ra) * NCX:(rb - ra) * NCX].rearrange("p (r c) -> p r c", r=nr)
                            rhs = uv[:, r0 - tau:rb - tau, 63:128]
                            last = e.matmul(outv, lhsT=KBv[:, tau, :], rhs=rhs, start=(tau == 0), stop=False,
                                            skip_group_check=True)
                        for r in range(ra, rb):
                            for j in range(4):
                                for par in range(2):
                                    g = ft * 8 + 2 * j + par
                                    fts = FTEv if par == 0 else FTOv
                                    last = e.matmul(pap[32 * j:32 * j + 32, (r - ra) * NCX:(r - ra + 1) * NCX],
                                                    lhsT=fts[:, r, 32 * j:32 * j + 32], rhs=XBFv[:, g, :],
                                                    start=False, stop=(r == rb - 1 and j == 3 and par == 1),
                                                    skip_group_check=True, tile_position=(0, 32 * j))
                        return last
                    S.op("tensor", f_y, reads=[bKB, bFTE, bFTO, bXBF, bU[ft]], writes=[pb])
                    nr = rb - ra
                    S.part("scalar", lambda e, pap=pap, ra=ra, rb=rb, nr=nr, ystv=ystv: e.activation(
                        out=ystv[:, :, ra:rb].rearrange("p c r -> p r c"),
                        in_=pap[:, 0:nr * NCX].rearrange("p (r c) -> p r c", r=nr), func=AF.Gelu_apprx_tanh),
                        [pb], bYST[ysl], ra == 0)
                S.op("sync", lambda e, ft=ft, ysl=ysl: e.dma_start(out=ya0_d[:, ft * YW:(ft + 1) * YW], in_=YST[ysl]),
                     reads=[bYST[ysl]], dma="ya", serial=False)
            ya_done = S.q["sync"][-1]
            if limit == "B":
                S.op("sync", lambda e: e.nop(), extra=[ya_done])
                return
            phaseU_bufs = [bFT32, bFTT, bFTE, bFTO, bKB, bDD, bXBF, bC, ch, bYST[0], bYST[1]] + bU

            o = BASE
            R1 = bv(o, 16 * NCOL); o += 8208
            R2o = o
            R2 = bv(o, 16 * NCOL); o += 8208
            R3o = o
            R3 = fv(o, 16 * NCOL); o += 16416
            SMo = o
            o += 2100
            assert o <= NW, o
            R1v = R1.rearrange("p (k n) -> p k n", k=16)
            R2v = R2.rearrange("p (k n) -> p k n", k=16)
            R3v = R3.rearrange("p (k n) -> p k n", k=16)
            YA0P = bv(R3o, 8 * NCOL).rearrange("p (k n) -> p k n", k=8)
            YAIN = bv(R3o + 4104, 8 * NCOL).rearrange("p (k n) -> p k n", k=8)
            YBIN = bv(R3o + 8208, 8 * NCOL).rearrange("p (k n) -> p k n", k=8)
            SPo = R3o + 12312
            XST = [fv(SMo, NCOL), fv(SMo + NCOL, NCOL)]
            GB = [bv(R2o + i * 4104, 8 * NCOL).rearrange("p (k n) -> p k n", k=8) for i in range(1)]
            EXT = fv(SMo, 1028)
            TT = fv(SMo + 1028, NCOL)
            OST = [fv(R2o + 4104 + i * NCOL, NCOL) for i in range(2)]

            bR1 = [Buf() for _ in range(16)]
            bR2 = [Buf() for _ in range(16)]
            bR3 = [Buf() for _ in range(16)]
            first_deps = []
            for b in phaseU_bufs:
                first_deps += b.wdeps()
            for b in bR1 + bR2 + bR3:
                b.w = list(first_deps)
            bXST = [Buf(), Buf()]
            for b in bXST:
                b.w = list(first_deps)
            for b in W.bufs:
                b.w = list(first_deps)
            bRSTD = Buf()
            xst_i = [0]

            def load_x(j, c0):
                sl = xst_i[0] % 2
                xst_i[0] += 1
                S.op("sync", lambda e, sl=sl, j=j, c0=c0: e.dma_start(out=XST[sl], in_=xT[j * P:(j + 1) * P, c0:c0 + NCOL]),
                     writes=[bXST[sl]], dma="xs%d" % sl)
                return sl

            def stats_and_rstd(sqv, bsq, sd_ap, bsd):
                banks = [PS.get() for _ in range(NBLK)]

                def f_st(e):
                    last = None
                    for k in range(16):
                        for b in range(NBLK):
                            last = e.matmul(banks[b][0][:, 0:BW], lhsT=ONES, rhs=sqv[:, k, b * BW:(b + 1) * BW],
                                            start=(k == 0), stop=(k == 15))
                    return last
                S.op("tensor", f_st, reads=list(bsq) + [bONES], writes=[bk[1] for bk in banks])
                for b in range(NBLK):
                    S.op("scalar", lambda e, b=b: e.activation(out=sd_ap[:, b * BW:(b + 1) * BW], in_=banks[b][0][:, 0:BW],
                                                                func=AF.Ln, bias=EPS, scale=1.0 / D),
                         reads=[banks[b][1]], writes=[], extra=bsd.wdeps() if b == 0 else bsd.w)
                    if b == 0:
                        bsd.w = [S.q["scalar"][-1]]
                        bsd.r = {}
                    else:
                        bsd.w.append(S.q["scalar"][-1])
                S.op("scalar", lambda e: e.activation(out=RSTD, in_=sd_ap, func=AF.Exp, scale=-0.5), reads=[bsd], writes=[bRSTD])

            def mm_tile(lhs_fn, rhs_fn, nk, reads):
                banks = [PS.get() for _ in range(NBLK)]

                def f(e):
                    last = None
                    for k in range(nk):
                        for b in range(NBLK):
                            last = e.matmul(banks[b][0][:, 0:BW], lhsT=lhs_fn(k), rhs=rhs_fn(k, b),
                                            start=(k == 0), stop=(k == nk - 1))
                    return last
                S.op("tensor", f, reads=reads, writes=[bk[1] for bk in banks])
                return banks

            def multi_write(buf, eng, first):
                o_ = S.q[eng][-1]
                if first:
                    buf.w = [o_]
                    buf.r = {}
                else:
                    buf.w.append(o_)

            xn_prefetched = [False]
            for ps_ in range(2):
                c0 = 2044 + ps_ * NCOL
                yc0 = 28 + ps_ * NCOL
                bYA0 = Buf()
                bYA0.w = bR3[0].wdeps()
                for j in range(1, 12):
                    bYA0.w += bR3[j].wdeps()
                S.op("sync", lambda e, yc0=yc0: e.dma_start(
                    out=YA0P, in_=ya0_d.rearrange("p (f n) -> p f n", f=8)[:, :, yc0:yc0 + NCOL]),
                    writes=[bYA0], extra=[ya_done], dma="yaL")
                bYAIN = [Buf() for _ in range(8)]
                bYBIN = [Buf() for _ in range(8)]
                for b in bYAIN + bYBIN:
                    b.w = list(bYA0.w[:-1])
                bSP = Buf()
                bSP.w = list(bYA0.w[:-1])
                bSD1 = Buf()
                bSD1.w = list(bSP.w)
                if not xn_prefetched[0]:
                    XS4a = [fv(R2o + i * NCOL, NCOL) for i in range(4)]
                    bXS4a = [Buf() for _ in range(4)]
                    for b in bXS4a:
                        for b2 in bR2:
                            b.w += b2.wdeps()
                    for j in range(16):
                        s4 = j % 4
                        S.op("sync", lambda e, s4=s4, j=j, c0=c0: e.dma_start(out=XS4a[s4], in_=xT[j * P:(j + 1) * P, c0:c0 + NCOL]),
                             writes=[bXS4a[s4]], dma="xq%d" % s4)
                        S.op("vector", lambda e, s4=s4, j=j, ps_=ps_: e.scalar_tensor_tensor(
                            out=R1v[:, j, :], in0=XS4a[s4], scalar=pcol(PV_NT + j), in1=RSA[:, ps_ * NCOL:(ps_ + 1) * NCOL],
                            op0=ALU.mult, op1=ALU.mult),
                            reads=[bXS4a[s4], bRSA, bPV], writes=[bR1[j]])
                    for b2 in bR2:
                        for b in bXS4a:
                            b2.w = b2.wdeps() + b.wdeps()
                        b2.r = {}
                if ps_ == 0:
                    dump("d_xn", R1, bR1)
                VT = [fv(SPo + i * NCOL, NCOL) for i in range(2)]
                CT = [fv(SPo + 2052 + i * NCOL, NCOL) for i in range(2)]
                EXTm = [fv(SMo, 1028), fv(SMo + 1028, 1028)]
                bVT = [Buf(), Buf()]
                bCT = [Buf(), Buf()]
                bEXTm = [Buf(), Buf()]
                for b in bVT + bCT:
                    b.w = bSD1.wdeps() + [S.q["vector"][-1]]
                for b in bEXTm:
                    b.w = bXST[0].wdeps() + bXST[1].wdeps()
                for ip in range(4):
                    wv, wb_, wk = W.get(w_in, 0, 16, 1024 + ip * 256, 256)
                    for t in range(2):
                        banks = mm_tile(lambda k, wv=wv, t=t: wv[:, k, t * 128:(t + 1) * 128],
                                        lambda k, b: R1v[:, k, b * BW:(b + 1) * BW], 16, bR1 + [wb_])
                        for b in range(NBLK):
                            S.op("scalar", lambda e, t=t, b=b, banks=banks: e.activation(
                                out=VT[t][:, b * BW:(b + 1) * BW], in_=banks[b][0][:, 0:BW], func=AF.Copy),
                                reads=[banks[b][1]], writes=[], extra=bVT[t].wdeps() if b == 0 else bVT[t].w)
                            multi_write(bVT[t], "scalar", b == 0)
                    W.release(wk)
                    wv, wb_, wk = W.get(w_in, 0, 16, 3072 + ip * 256, 256)
                    for t in range(2):
                        i = ip * 2 + t
                        banks = mm_tile(lambda k, wv=wv, t=t: wv[:, k, t * 128:(t + 1) * 128],
                                        lambda k, b: R1v[:, k, b * BW:(b + 1) * BW], 16, bR1 + [wb_])
                        S.op(POOL, lambda e, t=t, i=i: e.tensor_copy(out=EXTm[t][:, 0:2], in_=CVC[:, 2 * i:2 * i + 2]),
                             reads=[bCVC], writes=[bEXTm[t]])
                        for b in range(NBLK):
                            S.op("vector", lambda e, t=t, b=b, banks=banks: e.tensor_tensor(
                                out=EXTm[t][:, 2 + b * BW:2 + (b + 1) * BW], in0=VT[t][:, b * BW:(b + 1) * BW],
                                in1=banks[b][0][:, 0:BW], op=ALU.mult),
                                reads=[banks[b][1], bVT[t]], writes=[], extra=bEXTm[t].w)
                            bEXTm[t].w.append(S.q["vector"][-1])
                        S.op(POOL, lambda e, t=t, i=i: e.tensor_copy(out=CVC[:, 2 * i:2 * i + 2], in_=EXTm[t][:, NCOL:NCOL + 2]),
                             reads=[bEXTm[t]], writes=[bCVC])

                        def f_conv(e, t=t, i=i):
                            e.tensor_scalar(CT[t], EXTm[t][:, 0:NCOL], pcol(PV_CW + i), None, ALU.mult)
                            e.scalar_tensor_tensor(out=CT[t], in0=EXTm[t][:, 1:NCOL + 1], scalar=pcol(PV_CW + 8 + i),
                                                   in1=CT[t], op0=ALU.mult, op1=ALU.add)
                            return e.scalar_tensor_tensor(out=CT[t], in0=EXTm[t][:, 2:NCOL + 2], scalar=pcol(PV_CW + 16 + i),
                                                          in1=CT[t], op0=ALU.mult, op1=ALU.add)
                        S.op("vector", f_conv, reads=[bEXTm[t], bPV], writes=[bCT[t]])
                    W.release(wk)
                    wv, wb_, wk = W.get(w_in, 0, 16, 2048 + ip * 256, 256)
                    for t in range(2):
                        i = ip * 2 + t
                        banks = mm_tile(lambda k, wv=wv, t=t: wv[:, k, t * 128:(t + 1) * 128],
                                        lambda k, b: R1v[:, k, b * BW:(b + 1) * BW], 16, bR1 + [wb_])
                        for b in range(NBLK):
                            S.op("vector", lambda e, t=t, b=b, i=i, banks=banks: e.scalar_tensor_tensor(
                                out=YBIN[:, i, b * BW:(b + 1) * BW], in0=CT[t][:, b * BW:(b + 1) * BW],
                                scalar=pcol(PV_CB + i), in1=banks[b][0][:, 0:BW], op0=ALU.add, op1=ALU.mult),
                                reads=[banks[b][1], bCT[t], bPV], writes=[], extra=bYBIN[i].wdeps() if b == 0 else bYBIN[i].w)
                            multi_write(bYBIN[i], "vector", b == 0)
                    W.release(wk)
                SG = [fv(SPo + i * NCOL, NCOL) for i in range(4)]
                bSG = [Buf() for _ in range(4)]
                for b in bSG:
                    b.w = bVT[0].wdeps() + bVT[1].wdeps() + bCT[0].wdeps() + bCT[1].wdeps()
                for ih in range(2):
                    wv, wb_, wk = W.get(w_glu, 0, 8, ih * 512, 512)
                    for t in range(4):
                        i = ih * 4 + t
                        banks = mm_tile(lambda k, wv=wv, t=t: wv[:, k, t * 128:(t + 1) * 128],
                                        lambda k, b: YA0P[:, k, b * BW:(b + 1) * BW], 8, [bYA0, wb_])
                        sgi = i % 4
                        for b in range(NBLK):
                            S.op("scalar", lambda e, b=b, sgi=sgi, banks=banks: e.activation(
                                out=SG[sgi][:, b * BW:(b + 1) * BW], in_=banks[b][0][:, 0:BW], func=AF.Sigmoid),
                                reads=[banks[b][1]], writes=[], extra=bSG[sgi].wdeps() if b == 0 else bSG[sgi].w)
                            multi_write(bSG[sgi], "scalar", b == 0)
                        S.op("vector", lambda e, i=i, sgi=sgi: e.tensor_tensor(out=YAIN[:, i, :], in0=SG[sgi], in1=YA0P[:, i, :], op=ALU.mult),
                             reads=[bSG[sgi], bYA0], writes=[bYAIN[i]])
                    W.release(wk)
                if ps_ == 0:
                    dump("d_yb", bv(R3o + 8208, 8 * NCOL), bYBIN)
                    dump("d_ya", bv(R3o + 4104, 8 * NCOL), bYAIN)
                for jp in range(8):
                    j0 = jp * 2
                    wma, bma, wk = W.get(w_in, 0, 16, 4096 + jp * 256, 256)
                    for t in range(2):
                        ba, bba = SG[t], bSG[t]
                        banks = mm_tile(lambda k, t=t, wma=wma: wma[:, k, t * 128:(t + 1) * 128],
                                        lambda k, b: R1v[:, k, b * BW:(b + 1) * BW], 16, bR1 + [bma])
                        for b in range(NBLK):
                            S.part("scalar", lambda e, b=b, ba=ba, banks=banks: e.activation(
                                out=ba[:, b * BW:(b + 1) * BW], in_=banks[b][0][:, 0:BW], func=AF.Sigmoid),
                                [banks[b][1]], bba, b == 0)
                    W.release(wk)
                    if ps_ == 0 and jp == 0:
                        dump("d_sa", SG[0], [bSG[0]])
                    wso, bso, wk = W.get(w_so, 0, 8, jp * 256, 256)
                    for t in range(2):
                        ba, bba = SG[t], bSG[t]
                        banks = mm_tile(lambda k, t=t, wso=wso: wso[:, k, t * 128:(t + 1) * 128],
                                        lambda k, b: YAIN[:, k, b * BW:(b + 1) * BW], 8, bYAIN + [bso])
                        for b in range(NBLK):
                            S.part("vector", lambda e, b=b, ba=ba, banks=banks: e.tensor_tensor(
                                out=ba[:, b * BW:(b + 1) * BW], in0=ba[:, b * BW:(b + 1) * BW], in1=banks[b][0][:, 0:BW], op=ALU.mult),
                                [banks[b][1]], bba, False, extra=list(bba.r.values()))
                    W.release(wk)
                    wmb, bmb, wk = W.get(w_in, 0, 16, 6144 + jp * 256, 256)
                    for t in range(2):
                        bb, bbb = SG[2 + t], bSG[2 + t]
                        banks = mm_tile(lambda k, t=t, wmb=wmb: wmb[:, k, t * 128:(t + 1) * 128],
                                        lambda k, b: R1v[:, k, b * BW:(b + 1) * BW], 16, bR1 + [bmb])
                        for b in range(NBLK):
                            S.part("scalar", lambda e, b=b, bb=bb, banks=banks: e.activation(
                                out=bb[:, b * BW:(b + 1) * BW], in_=banks[b][0][:, 0:BW], func=AF.Sigmoid),
                                [banks[b][1]], bbb, b == 0)
                    W.release(wk)
                    wco, bco, wk = W.get(w_co, 0, 8, jp * 256, 256)
                    for t in range(2):
                        bb, bbb = SG[2 + t], bSG[2 + t]
                        banks = mm_tile(lambda k, t=t, wco=wco: wco[:, k, t * 128:(t + 1) * 128],
                                        lambda k, b: YBIN[:, k, b * BW:(b + 1) * BW], 8, bYBIN + [bco])
                        for b in range(NBLK):
                            S.part("vector", lambda e, b=b, bb=bb, banks=banks: e.tensor_tensor(
                                out=bb[:, b * BW:(b + 1) * BW], in0=bb[:, b * BW:(b + 1) * BW], in1=banks[b][0][:, 0:BW], op=ALU.mult),
                                [banks[b][1]], bbb, False, extra=list(bbb.r.values()))
                    W.release(wk)
                    for t in range(2):
                        j = j0 + t
                        if ps_ == 0 and j == 0:
                            dump("d_ba", SG[0], [bSG[0]])
                            dump("d_bb", SG[2], [bSG[2]])
                        S.op(POOL, lambda e, j=j, t=t: e.tensor_tensor(out=R2v[:, j, :], in0=SG[t], in1=SG[2 + t], op=ALU.add),
                             reads=[bSG[t], bSG[2 + t]], writes=[bR2[j]])
                if ps_ == 0:
                    dump("d_mg", R2, bR2)
                phaseM = [bYA0] + bYAIN + bYBIN + bSG + bVT + bCT
                r3guard = []
                for b in phaseM:
                    r3guard += b.wdeps()
                for b in bR3:
                    b.w = list(r3guard)
                    b.r = {}
                for b in bXST:
                    b.w = b.wdeps() + bEXTm[0].wdeps() + bEXTm[1].wdeps()
                    b.r = {}
                for jp in range(8):
                    wv, wb_, wk = W.get(w_o, 0, 16, jp * 256, 256)
                    for t in range(2):
                        j = jp * 2 + t
                        banks = mm_tile(lambda k, t=t, wv=wv: wv[:, k, t * 128:(t + 1) * 128],
                                        lambda k, b: R2v[:, k, b * BW:(b + 1) * BW], 16, bR2 + [wb_])
                        sl = load_x(j, c0)
                        for b in range(NBLK):
                            S.op("vector", lambda e, b=b, j=j, sl=sl, banks=banks: e.tensor_tensor(
                                out=R3v[:, j, b * BW:(b + 1) * BW], in0=XST[sl][:, b * BW:(b + 1) * BW], in1=banks[b][0][:, 0:BW], op=ALU.add),
                                reads=[banks[b][1], bXST[sl]], writes=[], extra=bR3[j].wdeps() if b == 0 else bR3[j].w)
                            multi_write(bR3[j], "vector", b == 0)
                        S.op("scalar", lambda e, j=j: e.activation(out=R1v[:, j, :], in_=R3v[:, j, :], func=AF.Square),
                             reads=[bR3[j]], writes=[bR1[j]])
                    W.release(wk)
                if ps_ == 0:
                    dump("d_h1", R3, bR3)
                SD2 = fv(R2o, NCOL)
                bSD2 = Buf()
                for b in bR2:
                    bSD2.w += b.wdeps()
                stats_and_rstd(R1v, bR1, SD2, bSD2)
                for j in range(16):
                    S.op("vector", lambda e, j=j: e.scalar_tensor_tensor(
                        out=R1v[:, j, :], in0=R3v[:, j, :], scalar=pcol(PV_NF + j), in1=RSTD, op0=ALU.mult, op1=ALU.mult),
                        reads=[bR3[j], bRSTD, bPV], writes=[bR1[j]])
                if ps_ == 0:
                    dump("d_hn", R1, bR1)
                bG = Buf()
                bG.w = bSD2.wdeps()
                bEXT, bTT = Buf(), Buf()
                bEXT.w = bXST[0].wdeps() + bXST[1].wdeps()
                bTT.w = list(bEXT.w)
                groups = [(0, 4), (4, 12), (12, 20), (20, 28), (28, 36), (36, 44)]
                for (fa, fb) in groups:
                    ng = fb - fa
                    bGf = [Buf() for _ in range(ng)]
                    for b in bGf:
                        b.w = bG.wdeps()
                    for fp in range(fa, fb, 2):
                        wa, bwa, wka = W.get(w_up, 0, 16, fp * 128, 256)
                        wbb, bwb, wkb = W.get(w_up, 0, 16, DFF + fp * 128, 256)
                        for t in range(2):
                            f = fp + t
                            fl = f - fa
                            banksA = mm_tile(lambda k, t=t, wa=wa: wa[:, k, t * 128:(t + 1) * 128],
                                             lambda k, b: R1v[:, k, b * BW:(b + 1) * BW], 16, bR1 + [bwa])
                            banksB = mm_tile(lambda k, t=t, wbb=wbb: wbb[:, k, t * 128:(t + 1) * 128],
                                             lambda k, b: R1v[:, k, b * BW:(b + 1) * BW], 16, bR1 + [bwb])
                            S.op(POOL, lambda e, f=f: e.tensor_copy(out=EXT[:, 0:2], in_=HAC[:, 2 * f:2 * f + 2]),
                                 reads=[bHAC], writes=[bEXT])
                            for b in range(NBLK):
                                S.op("scalar", lambda e, b=b, banksA=banksA: e.activation(
                                    out=EXT[:, 2 + b * BW:2 + (b + 1) * BW], in_=banksA[b][0][:, 0:BW], func=AF.Copy),
                                    reads=[banksA[b][1]], writes=[], extra=bEXT.w)
                                bEXT.w.append(S.q["scalar"][-1])
                            S.op(POOL, lambda e, f=f: e.tensor_copy(out=HAC[:, 2 * f:2 * f + 2], in_=EXT[:, NCOL:NCOL + 2]),
                                 reads=[bEXT], writes=[bHAC])

                            def f_conv2(e, f=f):
                                e.tensor_scalar(TT, EXT[:, 0:NCOL], pcol(PV_FW + f), None, ALU.mult)
                                e.scalar_tensor_tensor(out=TT, in0=EXT[:, 1:NCOL + 1], scalar=pcol(PV_FW + 44 + f),
                                                       in1=TT, op0=ALU.mult, op1=ALU.add)
                                return e.scalar_tensor_tensor(out=TT, in0=EXT[:, 2:NCOL + 2], scalar=pcol(PV_FW + 88 + f),
                                                              in1=TT, op0=ALU.mult, op1=ALU.add)
                            S.op("vector", f_conv2, reads=[bEXT, bPV], writes=[bTT])
                            S.op("scalar", lambda e, f=f: e.activation(out=TT, in_=TT, func=AF.Gelu_apprx_tanh, bias=pcol(PV_FB + f)),
                                 reads=[bPV], writes=[bTT])
                            for b in range(NBLK):
                                S.op("vector", lambda e, b=b, fl=fl, banksB=banksB: e.tensor_tensor(
                                    out=GB[0][:, fl, b * BW:(b + 1) * BW], in0=TT[:, b * BW:(b + 1) * BW], in1=banksB[b][0][:, 0:BW], op=ALU.mult),
                                    reads=[banksB[b][1], bTT], writes=[], extra=bGf[fl].wdeps() if b == 0 else bGf[fl].w)
                                multi_write(bGf[fl], "vector", b == 0)
                        W.release(wka)
                        W.release(wkb)
                    last_grp = (fb == FT_FF)
                    if last_grp:
                        stb = PS.hold_specific([5, 6, 7]) if os.environ.get("K_NOHOLD", "0") != "1" else [PS.get() + (0,) for _ in range(3)]
                        SQR = [bv(R2o + 6156 + i * 513, NCOL) for i in range(2)]
                        bSQR = [Buf(), Buf()]
                        for b in bSQR:
                            for b2 in bR2:
                                b.w += b2.wdeps()
                        pend = []
                        for b in bXST:
                            b.w = b.wdeps() + bEXT.wdeps() + bTT.wdeps()
                            b.r = {}

                        def stat_acc(j):
                            sl = j % 2

                            def f(e, j=j, sl=sl):
                                last = None
                                for b in range(NBLK):
                                    last = e.matmul(stb[b][0][:, 0:BW], lhsT=ONES, rhs=SQR[sl][:, b * BW:(b + 1) * BW],
                                                    start=(j == 0 or NOACC), stop=(j == 15 or NOACC), skip_group_check=True)
                                return last
                            S.op("tensor", f, reads=[bSQR[sl], bONES], writes=[sb_[1] for sb_ in stb])
                    for jq in range(4):
                        wd, bwd, wkd = W.get(w_dn, fa * P, ng, jq * 512, 512)
                        for t in range(4):
                            j = jq * 4 + t
                            banks = mm_tile(lambda k, t=t, wd=wd: wd[:, k, t * 128:(t + 1) * 128],
                                            lambda k, b: GB[0][:, k, b * BW:(b + 1) * BW], ng, bGf + [bwd])
                            if last_grp and ps_ == 0 and PF:
                                c1 = 2044 + NCOL
                                sl = j % 2
                                S.op("sync", lambda e, sl=sl, j=j, c1=c1: e.dma_start(out=XST[sl], in_=xT[j * P:(j + 1) * P, c1:c1 + NCOL]),
                                     writes=[bXST[sl]], dma="xs%d" % sl)
                            for b in range(NBLK):
                                S.op("vector", lambda e, b=b, j=j, banks=banks: e.tensor_tensor(
                                    out=R3v[:, j, b * BW:(b + 1) * BW], in0=R3v[:, j, b * BW:(b + 1) * BW], in1=banks[b][0][:, 0:BW], op=ALU.add),
                                    reads=[banks[b][1], bR3[j]], writes=[])
                                bR3[j].w.append(S.q["vector"][-1])
                            if last_grp:
                                if len(pend) >= 2:
                                    stat_acc(pend.pop(0))
                                S.op("scalar", lambda e, j=j: e.activation(out=SQR[j % 2], in_=R3v[:, j, :], func=AF.Square),
                                     reads=[bR3[j]], writes=[bSQR[j % 2]])
                                pend.append(j)
                                if ps_ == 0 and j >= 1 and PF:
                                    jj = j - 1
                                    S.op("vector", lambda e, jj=jj: e.scalar_tensor_tensor(
                                        out=R1v[:, jj, :], in0=XST[jj % 2], scalar=pcol(PV_NT + jj), in1=RSA[:, NCOL:2 * NCOL],
                                        op0=ALU.mult, op1=ALU.mult),
                                        reads=[bXST[jj % 2], bRSA, bPV], writes=[bR1[jj]])
                        W.release(wkd)
                    if last_grp:
                        while pend:
                            stat_acc(pend.pop(0))
                        if ps_ == 0 and PF:
                            S.op("vector", lambda e: e.scalar_tensor_tensor(
                                out=R1v[:, 15, :], in0=XST[1], scalar=pcol(PV_NT + 15), in1=RSA[:, NCOL:2 * NCOL],
                                op0=ALU.mult, op1=ALU.mult),
                                reads=[bXST[1], bRSA, bPV], writes=[bR1[15]])
                            xn_prefetched[0] = True
                    bG.w = []
                    bG.r = {}
                    for b in bGf:
                        bG.w += b.wdeps()
                if ps_ == 0:
                    dump("d_h2", R3, bR3)
                SD3 = fv(R2o, NCOL)
                bSD3 = Buf()
                bSD3.w = bG.wdeps()
                for b in range(NBLK):
                    S.part("scalar", lambda e, b=b: e.activation(out=SD3[:, b * BW:(b + 1) * BW], in_=stb[b][0][:, 0:BW],
                                                                func=AF.Ln, bias=EPS, scale=1.0 / D),
                           [stb[b][1]], bSD3, b == 0)
                if os.environ.get("K_NOHOLD", "0") != "1":
                    PS.unhold(stb)
                S.op("scalar", lambda e: e.activation(out=RSTD, in_=SD3, func=AF.Exp, scale=-0.5), reads=[bSD3], writes=[bRSTD])
                bOST = [Buf(), Buf()]
                for b in bOST:
                    b.w = bG.wdeps()
                bXS4 = []

                def out_tile(j):
                    sl = j % 2
                    S.op("vector", lambda e, j=j, sl=sl: e.scalar_tensor_tensor(
                        out=OST[sl], in0=R3v[:, j, :], scalar=pcol(PV_NL + j), in1=RSTD, op0=ALU.mult, op1=ALU.mult),
                        reads=[bR3[j], bRSTD, bPV], writes=[bOST[sl]])
                    if ps_ == 0:
                        S.op("sync", lambda e, j=j, sl=sl: e.dma_start(out=outT[j * P:(j + 1) * P, 0:NCOL - 4], in_=OST[sl][:, 4:NCOL]),
                             reads=[bOST[sl]], dma="out", serial=False)
                    else:
                        S.op("sync", lambda e, j=j, sl=sl: e.dma_start(out=outT[j * P:(j + 1) * P, NCOL - 4:TOK], in_=OST[sl]),
                             reads=[bOST[sl]], dma="out", serial=False)
                for j in (12, 13, 14, 15) + tuple(range(12)):
                    out_tile(j)
                for b in bXST:
                    b.w = b.wdeps() + [S.q["vector"][-1]]
                    b.r = {}
                for b in bR2:
                    b.w = b.wdeps() + bOST[0].wdeps() + bOST[1].wdeps() + bG.wdeps() + bSD3.wdeps() + bSQR[0].wdeps() + bSQR[1].wdeps()
                    b.r = {}
            last_out = S.q["sync"][-1]
            S.op("sync", lambda e: e.nop(), extra=[last_out])


        S0 = Sched()
        W0 = WeightStream(S0, slots, plan=None)
        program(S0, W0)
        S = Sched()
        W = WeightStream(S, slots, plan=W0.log)
        program(S, W)
        assert W.issued == len(W0.log), (W.issued, len(W0.log))
        S.finalize()

        @block.sync
        def _(e):
            S.emit("sync", e, sems)

        @block.gpsimd
        def _(e):
            S.emit("gpsimd", e, sems)

        @block.tensor
        def _(e):
            S.emit("tensor", e, sems)

        @block.scalar
        def _(e):
            S.emit("scalar", e, sems)

        @block.vector
        def _(e):
            S.emit("vector", e, sems)
    return nc


def _cols(v):
    v = np.asarray(v, np.float32).reshape(-1)
    n = v.size // P
    return np.ascontiguousarray(v.reshape(n, P).T)


def _prep_shared(inp):
    pv = np.zeros((P, NPV), np.float32)
    pv[:, PV_NT:PV_NT + 16] = _cols(inp["norm_tok"][0])
    pv[:, PV_NF:PV_NF + 16] = _cols(inp["norm_ffn"][0])
    pv[:, PV_NL:PV_NL + 16] = _cols(inp["norm_final"])
    for k in range(3):
        pv[:, PV_CW + 8 * k:PV_CW + 8 * k + 8] = _cols(inp["conv_w"][0, k])
        pv[:, PV_FW + 44 * k:PV_FW + 44 * k + 44] = _cols(inp["ffn_conv_w"][0, k])
    pv[:, PV_CB:PV_CB + 8] = _cols(inp["conv_b"][0])
    pv[:, PV_FB:PV_FB + 44] = _cols(inp["ffn_conv_b"][0])
    pv[:, PV_DS:PV_DS + 8] = _cols(inp["d_skip"][0])
    pv[:64, PV_SGN] = -1.0
    pv[64:, PV_SGN] = 1.0
    pv[:64, PV_NSGN] = 1.0
    pv[64:, PV_NSGN] = -1.0
    gl = (np.arange(P) // 16) % 2
    pv[:, PV_EV] = (gl == 0)
    pv[:, PV_OD] = (gl == 1)
    arT = np.asarray(inp["a_re"][0], np.float32).T
    aiT = np.asarray(inp["a_im"][0], np.float32).T
    ldt = np.broadcast_to(np.asarray(inp["log_dt"][0], np.float32)[None, :], (P, 64))
    s5a = np.concatenate([np.concatenate([arT, arT], 0), np.concatenate([aiT, aiT], 0), ldt], 1)
    br = np.asarray(inp["b_re"][0], np.float32).transpose(1, 0, 2).reshape(64, 1024)
    bi = np.asarray(inp["b_im"][0], np.float32).transpose(1, 0, 2).reshape(64, 1024)
    cr = np.asarray(inp["c_re"][0], np.float32).transpose(2, 0, 1).reshape(64, 1024)
    ci = np.asarray(inp["c_im"][0], np.float32).transpose(2, 0, 1).reshape(64, 1024)
    s5b = np.concatenate([np.concatenate([br, bi], 0), np.concatenate([bi, br], 0)], 1)
    s5c = np.concatenate([np.concatenate([cr, ci], 0), np.concatenate([ci, cr], 0)], 1)
    ident = np.eye(P, dtype=np.float32)
    perm = np.roll(ident, 64, axis=0)
    blk = np.arange(P) // 16
    bdm = (blk[:, None] == blk[None, :]).astype(np.float32)
    ecol = np.broadcast_to((gl == 0).astype(np.float32)[None, :], (P, P))
    ocol = np.broadcast_to((gl == 1).astype(np.float32)[None, :], (P, P))
    cst = np.concatenate([ident, perm, bdm, ecol, ocol], 1)
    sh = {
        "w_in": np.ascontiguousarray(inp["w_in"][0], np.float32),
        "w_glu": np.ascontiguousarray(inp["w_glu"][0], np.float32),
        "w_ssm_out": np.ascontiguousarray(inp["w_ssm_out"][0], np.float32),
        "w_conv_out": np.ascontiguousarray(inp["w_conv_out"][0], np.float32),
        "w_o": np.ascontiguousarray(inp["w_o"][0], np.float32),
        "w_up": np.ascontiguousarray(inp["w_up"][0], np.float32),
        "w_down": np.ascontiguousarray(inp["w_down"][0], np.float32),
        "pvec": pv,
        "s5a": np.ascontiguousarray(s5a, np.float32),
        "s5b": np.ascontiguousarray(s5b, np.float32),
        "s5c": np.ascontiguousarray(s5c, np.float32),
        "cst": np.ascontiguousarray(cst, np.float32),
    }
    return sh


def _in_maps(inp, cores):
    sh = _prep_shared(inp)
    x = np.asarray(inp["x"], np.float32)
    maps = []
    for i in cores:
        b, half = i // 2, i % 2
        xa = np.zeros((2 * TOK, D), np.float32)
        if half == 1:
            xa[:] = x[b]
        else:
            xa[TOK:] = x[b, :TOK]
        m = dict(sh)
        m["xT"] = np.ascontiguousarray(xa.T)
        maps.append(m)
    return maps


_NC_CACHE = {}


def kernel(**inputs):
    if "nc" not in _NC_CACHE:
        _NC_CACHE["nc"] = build()
    nc = _NC_CACHE["nc"]
    maps = _in_maps(inputs, list(range(8)))
    res = run_bass_kernel_spmd(nc, maps, core_ids=list(range(8)))
    out = np.empty((4, 2 * TOK, D), np.float32)
    for i in range(8):
        b, half = i // 2, i % 2
        out[b, half * TOK:(half + 1) * TOK, :] = res.results[i]["outT"].T
    return out
```

```python
import math
import os
POOL = os.environ.get("K_POOL", "gpsimd")
PF = os.environ.get("K_PF", "1") == "1"
NOACC = os.environ.get("K_NOACC", "0") == "1"
POOL_PSUM = "vector"
import numpy as np
import concourse.bass as bass
import concourse.mybir as mybir
from concourse.bass_utils import run_bass_kernel_spmd

F32 = mybir.dt.float32
BF16 = mybir.dt.bfloat16
AF = mybir.ActivationFunctionType
ALU = mybir.AluOpType

P = 128
D = 2048
KT = 16
DFF = 5632
FT_FF = 44
TOK = 2048
NCOL = 1026
NBLK = 3
BW = 342
TC = 32
NCH = 128
NCX = 65
YW = NCX * TC
EPS = 1e-6
ENGS = ["sync", "gpsimd", "tensor", "scalar", "vector"]
NSLOT = 4
INORDER = set(os.environ.get("K_INORDER", "tensor,vector,scalar").split(","))
SLOTW = 2048

PV_NT, PV_NF, PV_NL = 0, 16, 32
PV_CW, PV_CB = 48, 72
PV_FW, PV_FB = 80, 212
PV_DS = 256
PV_SGN, PV_NSGN, PV_EV, PV_OD = 264, 265, 266, 267
NPV = 268


class Op:
    __slots__ = ("eng", "fn", "deps", "dma_sem", "flag", "count", "pos")

    def __init__(self, eng, fn, deps, dma_sem=None):
        self.eng = eng
        self.fn = fn
        self.deps = deps
        self.dma_sem = dma_sem
        self.flag = False
        self.count = None
        self.pos = 0


class Buf:
    __slots__ = ("w", "r")

    def __init__(self):
        self.w = []
        self.r = {}

    def rdeps(self):
        return list(self.w)

    def wdeps(self):
        return list(self.w) + list(self.r.values())


class Sched:
    def __init__(self):
        self.q = {e: [] for e in ENGS}
        self.dma_counts = {}
        self.last_dma = {}
        self.npos = 0

    def op(self, eng, fn, reads=(), writes=(), extra=(), dma=None, serial=True):
        deps = []
        if dma is not None and serial and dma in self.last_dma:
            deps.append(self.last_dma[dma])
        for b in reads:
            deps += b.rdeps()
        for b in writes:
            deps += b.wdeps()
        deps += [d for d in extra if d is not None]
        o = Op(eng, fn, deps, dma)
        self.npos += 1
        o.pos = self.npos
        if dma is not None:
            self.dma_counts[dma] = self.dma_counts.get(dma, 0) + 16
            o.count = self.dma_counts[dma]
            self.last_dma[dma] = o
        self.q[eng].append(o)
        for b in reads:
            key = eng if dma is None else ("dma", dma, o.pos)
            b.r[key] = o
        for b in writes:
            b.w = [o]
            b.r = {}
        return o

    def part(self, eng, fn, reads, buf, first, extra=()):
        ex = list(extra) + (buf.wdeps() if first else list(buf.w))
        o = self.op(eng, fn, reads=reads, extra=ex)
        if first:
            buf.w = [o]
            buf.r = {}
        else:
            buf.w.append(o)
        return o

    def finalize(self):
        for e in ENGS:
            for o in self.q[e]:
                for d in o.deps:
                    if d.dma_sem is None:
                        if d.eng == o.eng and d.eng in INORDER:
                            continue
                        d.flag = True
        for e in ENGS:
            c = 0
            for o in self.q[e]:
                if o.dma_sem is None and o.flag:
                    c += 1
                    o.count = c

    def emit(self, eng_name, e, sems):
        waited = {}
        for o in self.q[eng_name]:
            need = {}
            for d in o.deps:
                if d.dma_sem is not None:
                    key = "D_" + d.dma_sem
                else:
                    if d.eng == eng_name and eng_name in INORDER:
                        continue
                    key = "E_" + d.eng
                if need.get(key, 0) < d.count:
                    need[key] = d.count
            for key, cnt in need.items():
                if waited.get(key, 0) < cnt:
                    e.wait_ge(sems[key], cnt)
                    waited[key] = cnt
            ins = o.fn(e)
            if o.dma_sem is not None:
                ins.then_inc(sems["D_" + o.dma_sem], 16)
            elif o.flag:
                ins.then_inc(sems["E_" + eng_name], 1)


class PsumRing:
    def __init__(self, ps):
        self.aps = [ps[:, 512 * i:512 * (i + 1)] for i in range(8)]
        self.bufs = [Buf() for _ in range(8)]
        self.held = set()
        self.i = 0

    def get(self):
        while True:
            k = self.i % 8
            self.i += 1
            if k not in self.held:
                return self.aps[k], self.bufs[k]

    def hold(self, n):
        out = []
        while len(out) < n:
            k = self.i % 8
            self.i += 1
            if k not in self.held:
                self.held.add(k)
                out.append((self.aps[k], self.bufs[k], k))
        return out

    def hold_specific(self, ks):
        out = []
        for k in ks:
            self.held.add(k)
            out.append((self.aps[k], self.bufs[k], k))
        return out

    def unhold(self, banks):
        for b in banks:
            self.held.discard(b[2])


class WeightStream:
    def __init__(self, S, slots, plan=None):
        self.S = S
        self.slots = slots
        self.bufs = [Buf() for _ in slots]
        self.plan = plan
        self.log = []
        self.issued = 0
        self.k = 0
        self.released = set()
        self.nrel = 0

    def _issue(self, m):
        w, r0, nk, c0, ncol = self.plan[m]
        s = m % NSLOT
        dst = self.slots[s][:, 0:nk * ncol].rearrange("p (k c) -> p k c", k=nk)
        src = w[r0:r0 + nk * P, c0:c0 + ncol].rearrange("(k p) c -> p k c", p=P)
        self.S.op("gpsimd", lambda e, dst=dst, src=src: e.dma_start(out=dst, in_=src),
                  writes=[self.bufs[s]], dma="w%d" % s)

    def _pump(self):
        if self.plan is None:
            return
        while self.issued < len(self.plan) and self.issued < self.nrel + NSLOT:
            self._issue(self.issued)
            self.issued += 1

    def release(self, k):
        self.released.add(k)
        while self.nrel in self.released:
            self.nrel += 1
        self._pump()

    def get(self, w, r0, nk, c0, ncol):
        assert nk * ncol <= 2 * SLOTW
        spec = (w, r0, nk, c0, ncol)
        k = self.k
        self.k += 1
        s = k % NSLOT
        view = self.slots[s][:, 0:nk * ncol].rearrange("p (k c) -> p k c", k=nk)
        if self.plan is None:
            self.log.append(spec)
            return view, self.bufs[s], k
        assert self.plan[k][1:] == spec[1:]
        self._pump()
        assert k < self.issued, "too many weight chunks held at once"
        return view, self.bufs[s], k


def build(debug=False, limit=None):
    nc = bass.Bass("TRN2", target_bir_lowering=False)
    dt_ = nc.dram_tensor
    xT = dt_("xT", [D, 2 * TOK], F32, kind="ExternalInput").ap()
    w_in = dt_("w_in", [D, 8192], F32, kind="ExternalInput").ap()
    w_glu = dt_("w_glu", [1024, 1024], F32, kind="ExternalInput").ap()
    w_so = dt_("w_ssm_out", [1024, D], F32, kind="ExternalInput").ap()
    w_co = dt_("w_conv_out", [1024, D], F32, kind="ExternalInput").ap()
    w_o = dt_("w_o", [D, D], F32, kind="ExternalInput").ap()
    w_up = dt_("w_up", [D, 2 * DFF], F32, kind="ExternalInput").ap()
    w_dn = dt_("w_down", [DFF, D], F32, kind="ExternalInput").ap()
    pvec_d = dt_("pvec", [P, NPV], F32, kind="ExternalInput").ap()
    s5a_d = dt_("s5a", [P, 3 * 64], F32, kind="ExternalInput").ap()
    s5b_d = dt_("s5b", [P, 2 * 1024], F32, kind="ExternalInput").ap()
    s5c_d = dt_("s5c", [P, 2 * 1024], F32, kind="ExternalInput").ap()
    cst_d = dt_("cst", [P, 5 * 128], F32, kind="ExternalInput").ap()
    outT = dt_("outT", [D, TOK], F32, kind="ExternalOutput").ap()
    ya0_d = dt_("ya0_d", [P, 8 * YW], BF16, kind=("ExternalOutput" if debug else "Internal")).ap()

    NW = 48256
    dbg_t = {}
    if debug:
        for nm, n, dty in [("d_xn", 16 * NCOL, BF16), ("d_yb", 8 * NCOL, BF16), ("d_ya", 8 * NCOL, BF16),
                           ("d_mg", 16 * NCOL, BF16), ("d_h1", 16 * NCOL, F32), ("d_hn", 16 * NCOL, BF16),
                           ("d_h2", 16 * NCOL, F32), ("d_ba", NCOL, F32), ("d_bb", NCOL, F32), ("d_sa", NCOL, F32)]:
            dbg_t[nm] = dt_(nm, [P, n], dty, kind="ExternalOutput").ap()
    sem_names = ["E_" + e for e in ENGS] + ["D_w%d" % i for i in range(NSLOT)] + \
        ["D_cst", "D_xs0", "D_xs1", "D_ya", "D_out", "D_wu", "D_yaL", "D_dbg", "D_xq0", "D_xq1", "D_xq2", "D_xq3"]

    import contextlib
    with contextlib.ExitStack() as es:
        AR = es.enter_context(nc.sbuf_tensor("arena", [P, NW], F32))
        PSUM = es.enter_context(nc.psum_tensor("ps", [P, 4096], F32))
        sems = {n: es.enter_context(nc.semaphore(n)) for n in sem_names}
        block = es.enter_context(nc.Block())

        def fv(off, n):
            return AR[:, off:off + n]

        def bv(off, nbf):
            return AR[:, off:off + nbf // 2].bitcast(BF16)

        o = 0
        PV = fv(o, NPV); o += NPV
        ONES = bv(o, 128); o += 64
        CVC = fv(o, 16); o += 16
        HAC = fv(o, 88); o += 88
        RSTD = fv(o, NCOL); o += NCOL
        IDENT = fv(o, 128); o += 128
        RSA = fv(o, 2 * NCOL); o += 2 * NCOL
        o_persist = o
        RING0 = 3648
        assert o_persist <= RING0
        slots = [bv(RING0 + i * SLOTW, 2 * SLOTW) for i in range(NSLOT)]
        BASE = RING0 + NSLOT * SLOTW

        def pcol(c):
            return PV[:, c:c + 1]

        def program(S, W):
            PS = PsumRing(PSUM)

            def dump(nm, ap, bufs):
                if debug:
                    S.op("sync", lambda e: e.dma_start(out=dbg_t[nm], in_=ap), reads=bufs, dma="dbg")

            bPV, bONES, bCST = Buf(), Buf(), Buf()
            S.op("sync", lambda e: e.dma_start(out=PV, in_=pvec_d), writes=[bPV], dma="cst")
            S.op("sync", lambda e: e.dma_start(out=IDENT, in_=cst_d[:, 0:128]), writes=[bCST], dma="cst")
            S.op("vector", lambda e: e.memset(ONES, 1.0), writes=[bONES])
            bCVC, bHAC = Buf(), Buf()
            S.op("vector", lambda e: e.memset(CVC, 0.0), writes=[bCVC])
            S.op("vector", lambda e: e.memset(HAC, 0.0), writes=[bHAC])

            o = RING0
            UALL = bv(o, 8 * 4096); o += 16384
            UALLr = UALL.rearrange("p (f s c) -> p f s c", f=8, s=TC)
            o_u = o
            WU = bv(o, 16 * 1024); o += 8192
            WUv = WU.rearrange("p (k c) -> p k c", k=16)
            XS = [fv(o + i * 4096, 4096) for i in range(2)]; o += 8192
            SQB = bv(o, 4096); o += 2048
            XNB = [bv(o + i * 2048, 4096) for i in range(2)]; o += 4096
            SD = fv(o, 256); o += 256
            RS = fv(o, 256); o += 256
            assert o <= NW, o
            bWUq = [Buf() for _ in range(4)]
            for i in range(4):
                S.op("gpsimd", lambda e, i=i: e.dma_start(
                    out=WUv[:, :, i * 256:(i + 1) * 256],
                    in_=w_in[:, i * 256:(i + 1) * 256].rearrange("(k p) c -> p k c", p=P)),
                    writes=[bWUq[i]], dma="wu")
            bWU = Buf()
            bWU.w = [S.q["gpsimd"][-1]]
            xTv = xT.rearrange("(k p) n -> p k n", p=P)
            bXS = [Buf(), Buf()]
            bSQB, bSD, bRS = Buf(), Buf(), Buf()
            bXNB = [Buf(), Buf()]
            bU = [Buf() for _ in range(8)]
            UB = 256
            bRSA = Buf()
            NB_U = 2 * TOK // UB

            def u1_front_a(blk):
                sl = blk % 2
                c0 = blk * UB
                xs3 = XS[sl].rearrange("p (k n) -> p k n", k=16)
                S.op("sync", lambda e, xs3=xs3, c0=c0: e.dma_start(out=xs3, in_=xTv[:, :, c0:c0 + UB]),
                     writes=[bXS[sl]], dma="xs%d" % sl)
                S.op("scalar", lambda e, sl=sl: e.activation(out=SQB, in_=XS[sl], func=AF.Square),
                     reads=[bXS[sl]], writes=[bSQB])

            def u1_front_b(blk):
                sl = blk % 2
                c0 = blk * UB
                xs3 = XS[sl].rearrange("p (k n) -> p k n", k=16)
                pap, pb = PS.get()
                sq3 = SQB.rearrange("p (k n) -> p k n", k=16)

                def f_stat(e, pap=pap, sq3=sq3):
                    last = None
                    for k in range(16):
                        last = e.matmul(pap[:, 0:UB], lhsT=ONES, rhs=sq3[:, k, :], start=(k == 0), stop=(k == 15))
                    return last
                S.op("tensor", f_stat, reads=[bSQB, bONES], writes=[pb])
                S.op("scalar", lambda e, pap=pap: e.activation(out=SD, in_=pap[:, 0:UB], func=AF.Ln,
                                                                bias=EPS, scale=1.0 / D),
                     reads=[pb], writes=[bSD])
                S.op("scalar", lambda e: e.activation(out=RS, in_=SD, func=AF.Exp, scale=-0.5), reads=[bSD], writes=[bRS])
                t0 = max(c0, 2044)
                if t0 < c0 + UB:
                    S.part(POOL, lambda e, t0=t0, c0=c0: e.tensor_copy(out=RSA[:, t0 - 2044:c0 + UB - 2044], in_=RS[:, t0 - c0:UB]),
                           [bRS], bRSA, t0 == 2044)
                xn3 = XNB[sl].rearrange("p (k n) -> p k n", k=16)

                def f_xn(e, xs3=xs3, xn3=xn3):
                    last = None
                    for k in range(16):
                        last = e.scalar_tensor_tensor(out=xn3[:, k, :], in0=xs3[:, k, :], scalar=pcol(PV_NT + k),
                                                      in1=RS, op0=ALU.mult, op1=ALU.mult)
                    return last
                S.op("vector", f_xn, reads=[bXS[sl], bRS, bPV], writes=[bXNB[sl]])

            def u1_back(blk, fts):
                sl = blk % 2
                c0 = blk * UB
                xn3 = XNB[sl].rearrange("p (k n) -> p k n", k=16)
                for ft in fts:
                    pap, pb = PS.get()

                    def f_u(e, pap=pap, ft=ft, xn3=xn3):
                        last = None
                        for k in range(16):
                            last = e.matmul(pap[:, 0:UB], lhsT=WUv[:, k, ft * 128:(ft + 1) * 128], rhs=xn3[:, k, :],
                                            start=(k == 0), stop=(k == 15))
                        return last
                    S.op("tensor", f_u, reads=[bXNB[sl], bWUq[ft // 2]], writes=[pb])
                    cb = c0 // TC
                    ncb = UB // TC
                    src_ = pap[:, 0:UB].rearrange("p (c s) -> p s c", s=TC)
                    if ft % 2 == 0:
                        S.op("scalar", lambda e, src_=src_, ft=ft, cb=cb, ncb=ncb: e.activation(
                            out=UALLr[:, ft, :, cb:cb + ncb], in_=src_, func=AF.Copy),
                            reads=[pb], writes=[bU[ft]])
                    else:
                        S.op("vector", lambda e, src_=src_, ft=ft, cb=cb, ncb=ncb: e.tensor_copy(
                            out=UALLr[:, ft, :, cb:cb + ncb], in_=src_),
                            reads=[pb], writes=[bU[ft]])
            u1_front_a(0)
            u1_front_b(0)
            for blk in range(NB_U):
                if blk + 1 < NB_U:
                    u1_front_a(blk + 1)
                u1_back(blk, range(0, 4))
                if blk + 1 < NB_U:
                    u1_front_b(blk + 1)
                u1_back(blk, range(4, 8))
            bU1_done = [bXS[0], bXS[1], bSQB, bXNB[0], bXNB[1], bWU, bSD, bRS]
            if limit == "U1":
                return

            o = o_u
            SALL = fv(o, 64 * 128); o += 8192
            SALLv = SALL.rearrange("p (g c) -> p g c", g=64)
            SM = [fv(o + i * 64, 64) for i in range(26)]; o += 26 * 64
            (tAR, tAI, tLDT, tDT, tDAR, tDAI, tMAG, tS16, tC16, tZR, tZI, t1, t2, t3, tNR, tDEN, tFR, tFI, tF2,
             tLR, tLI, tA2r, tNA2r, tx1, tx2, tx3) = SM
            BST = fv(o, 1024); o += 1024
            PWR = fv(o, 33 * 64); o += 2112
            PWI = fv(o, 33 * 64); o += 2112
            PWRv = PWR.rearrange("p (s g) -> p s g", s=33)
            PWIv = PWI.rearrange("p (s g) -> p s g", s=33)
            XBF = bv(o, 64 * NCX + 64); o += (64 * NCX + 64) // 2
            XBFv = XBF[:, 0:64 * NCX].rearrange("p (g c) -> p g c", g=64)
            o_dead = o
            b1o = o
            B1 = fv(o, 1024); o += 1024
            B2 = fv(o, 1024); o += 1024
            BSW = fv(o, 1024); o += 1024
            EST = fv(o, 2048); o += 2048
            TMPE = fv(o, 2048); o += 2048
            ETE = bv(b1o, 4096)
            ETO = bv(o, 4096); o += 2048
            ZA = fv(o, 128); o += 128
            ZB = fv(o, 128); o += 128
            ZT1 = fv(o, 128); o += 128
            ZT2 = fv(o, 128); o += 128
            A1W = fv(o, 128); o += 128
            A2W = fv(o, 128); o += 128
            PERM = fv(o, 128); o += 128
            o_sweepA_end = o
            assert o <= NW, o
            ch = Buf()
            for b in bU1_done:
                ch.w += b.wdeps()
            ch.w = list(ch.w)
            S.op("sync", lambda e: e.dma_start(out=fv(o_u + 8192, 192), in_=s5a_d), writes=[ch], dma="cst")
            S.op("sync", lambda e: e.dma_start(out=fv(b1o, 2048), in_=s5b_d), writes=[ch], dma="cst")
            S.op("sync", lambda e: e.dma_start(out=PERM, in_=cst_d[:, 128:256]), writes=[ch], dma="cst")

            def V(fn):
                return S.op("vector", fn, writes=[ch])

            def A(fn):
                return S.op("scalar", fn, writes=[ch])

            def vtt(out, a, b, op):
                return V(lambda e: e.tensor_tensor(out=out, in0=a, in1=b, op=op))

            A(lambda e: e.activation(out=tDT, in_=tLDT, func=AF.Exp))
            vtt(tDAR, tDT, tAR, ALU.mult)
            vtt(tDAI, tDT, tAI, ALU.mult)
            A(lambda e: e.activation(out=tMAG, in_=tDAR, func=AF.Exp, scale=1.0 / 16))
            A(lambda e: e.activation(out=tS16, in_=tDAI, func=AF.Sin, scale=1.0 / 16))
            V(lambda e: e.tensor_scalar(tx1, tDAI, 1.0 / 16, math.pi / 2, ALU.mult, ALU.add))
            A(lambda e: e.activation(out=tC16, in_=tx1, func=AF.Sin))
            vtt(tZR, tMAG, tC16, ALU.mult)
            vtt(tZI, tMAG, tS16, ALU.mult)

            def csq(zr, zi):
                vtt(t1, zr, zr, ALU.mult)
                vtt(t2, zi, zi, ALU.mult)
                vtt(t3, zr, zi, ALU.mult)
                vtt(zr, t1, t2, ALU.subtract)
                V(lambda e: e.tensor_scalar(zi, t3, 2.0, None, ALU.mult))
            for _ in range(4):
                csq(tZR, tZI)
            V(lambda e: e.tensor_scalar(tNR, tZR, -1.0, None, ALU.add))
            vtt(t1, tAR, tAR, ALU.mult)
            vtt(t2, tAI, tAI, ALU.mult)
            vtt(tDEN, t1, t2, ALU.add)
            V(lambda e: e.reciprocal(out=tDEN, in_=tDEN))
            vtt(t1, tNR, tAR, ALU.mult)
            vtt(t2, tZI, tAI, ALU.mult)
            vtt(t1, t1, t2, ALU.add)
            vtt(tFR, t1, tDEN, ALU.mult)
            vtt(t1, tZI, tAR, ALU.mult)
            vtt(t2, tNR, tAI, ALU.mult)
            vtt(t1, t1, t2, ALU.subtract)
            vtt(tFI, t1, tDEN, ALU.mult)
            V(lambda e: e.tensor_scalar(tF2, tFI, pcol(PV_SGN), None, ALU.mult))
            B1v = B1.rearrange("p (g h) -> p g h", g=64)
            B2v = B2.rearrange("p (g h) -> p g h", g=64)
            BSTv = BST.rearrange("p (g h) -> p g h", g=64)
            BSWv = BSW.rearrange("p (g h) -> p g h", g=64)

            def bc_h(t):
                return t.unsqueeze(2).broadcast_to([P, 64, 16])
            vtt(BSTv, B1v, bc_h(tFR), ALU.mult)
            vtt(BSWv, B2v, bc_h(tF2), ALU.mult)
            vtt(BSTv, BSTv, BSWv, ALU.add)
            vtt(BSWv, B2v, bc_h(tFR), ALU.mult)
            vtt(B2v, B1v, bc_h(tF2), ALU.mult)
            vtt(BSWv, BSWv, B2v, ALU.subtract)
            V(lambda e: e.tensor_scalar(BSW, BSW, pcol(PV_SGN), None, ALU.mult))
            V(lambda e: e.memset(PWRv[:, 0, :], 1.0))
            V(lambda e: e.memset(PWIv[:, 0, :], 0.0))
            V(lambda e: e.tensor_copy(out=tLR, in_=tZR))
            V(lambda e: e.tensor_copy(out=tLI, in_=tZI))
            TA = TMPE[:, 0:1024]
            TB = TMPE[:, 1024:2048]
            n = 1
            while n <= 16:
                def bcs(t, n=n):
                    return t.unsqueeze(1).broadcast_to([P, n, 64])
                ta = TA.rearrange("p (s g) -> p s g", g=64)[:, 0:n, :]
                tb = TB.rearrange("p (s g) -> p s g", g=64)[:, 0:n, :]
                vtt(ta, PWRv[:, 0:n, :], bcs(tLR), ALU.mult)
                vtt(tb, PWIv[:, 0:n, :], bcs(tLI), ALU.mult)
                vtt(PWRv[:, n:2 * n, :], ta, tb, ALU.subtract)
                vtt(ta, PWRv[:, 0:n, :], bcs(tLI), ALU.mult)
                vtt(tb, PWIv[:, 0:n, :], bcs(tLR), ALU.mult)
                vtt(PWIv[:, n:2 * n, :], ta, tb, ALU.add)
                csq(tLR, tLI)
                n *= 2
            V(lambda e: e.tensor_copy(out=PWRv[:, 32, :], in_=tLR))
            V(lambda e: e.tensor_copy(out=PWIv[:, 32, :], in_=tLI))
            V(lambda e: e.tensor_copy(out=A1W[:, 0:64], in_=tLR))
            V(lambda e: e.tensor_copy(out=A1W[:, 64:128], in_=tLR))
            V(lambda e: e.tensor_scalar(A2W[:, 0:64], tLI, pcol(PV_SGN), None, ALU.mult))
            V(lambda e: e.tensor_scalar(A2W[:, 64:128], tLI, pcol(PV_NSGN), None, ALU.mult))
            bPW = ch

            if limit == "DERIVE":
                return
            bEST, bTMPE = Buf(), Buf()
            bETE = [Buf(), Buf()]
            bETO = [Buf(), Buf()]
            for b in [bEST, bTMPE] + bETE + bETO:
                b.w = list(ch.w)
            ESTv = EST.rearrange("p (q c) -> p q c", q=16)
            ETEv = ETE.rearrange("p (q c) -> p q c", q=32)
            ETOv = ETO.rearrange("p (q c) -> p q c", q=32)
            bSALL = Buf()
            sbanks = PS.hold(4)
            sbuf2 = []
            for _ in range(4):
                b_ = Buf()
                sbuf2.append([b_, b_])

            def salle_evac(ft):
                hh = ft % 2
                for j in range(4):
                    g0 = ft * 8 + 2 * j
                    dst = SALL[:, g0 * 128:(g0 + 2) * 128]
                    srcp = sbanks[j][0][:, hh * 256:(hh + 1) * 256]
                    firstw = (ft == 0 and j == 0)
                    if j % 2 == 0:
                        S.part("scalar", lambda e, dst=dst, srcp=srcp: e.activation(out=dst, in_=srcp, func=AF.Copy),
                               [sbuf2[j][hh]], bSALL, firstw, extra=list(ch.w) if firstw else [])
                    else:
                        S.part("vector", lambda e, dst=dst, srcp=srcp: e.tensor_copy(out=dst, in_=srcp),
                               [sbuf2[j][hh]], bSALL, firstw)
            pending = None
            SB_PERSIST = os.environ.get("K_SB", "persist") == "persist"
            for ft in range(8):
                hh = ft % 2
                for hf in range(2):
                    q0 = hf * 16
                    pw_r = PWRv[:, q0:q0 + 16, ft * 8:(ft + 1) * 8].unsqueeze(3).broadcast_to([P, 16, 8, 16])
                    pw_i = PWIv[:, q0:q0 + 16, ft * 8:(ft + 1) * 8].unsqueeze(3).broadcast_to([P, 16, 8, 16])
                    bst = BST[:, ft * 128:(ft + 1) * 128].rearrange("p (g h) -> p g h", g=8).unsqueeze(1).broadcast_to([P, 16, 8, 16])
                    bsw = BSW[:, ft * 128:(ft + 1) * 128].rearrange("p (g h) -> p g h", g=8).unsqueeze(1).broadcast_to([P, 16, 8, 16])
                    e4 = EST.rearrange("p (q g h) -> p q g h", q=16, g=8)
                    t4 = TMPE.rearrange("p (q g h) -> p q g h", q=16, g=8)
                    S.op("vector", lambda e, e4=e4, pw_r=pw_r, bst=bst: e.tensor_tensor(out=e4, in0=pw_r, in1=bst, op=ALU.mult),
                         reads=[bPW], writes=[bEST])
                    S.op(POOL, lambda e, t4=t4, pw_i=pw_i, bsw=bsw: e.tensor_tensor(out=t4, in0=pw_i, in1=bsw, op=ALU.mult),
                         reads=[bPW], writes=[bTMPE])
                    S.op("vector", lambda e: e.tensor_tensor(out=EST, in0=EST, in1=TMPE, op=ALU.add),
                         reads=[bTMPE], writes=[bEST])
                    for qq in range(4):
                        pap, pb = PS.get()

                        def f_tr(e, pap=pap, qq=qq):
                            last = None
                            for i in range(4):
                                last = e.transpose(out=pap[:, i * 128:(i + 1) * 128], in_=ESTv[:, qq * 4 + i, :], identity=IDENT)
                            return last
                        S.op("tensor", f_tr, reads=[bEST, bCST], writes=[pb])
                        qa = q0 + qq * 4
                        ete_op = S.part("vector", lambda e, pap=pap, qa=qa: e.tensor_scalar(
                            ETE[:, qa * 128:(qa + 4) * 128], pap[:, 0:512], pcol(PV_EV), None, ALU.mult),
                            [pb, bPV], bETE[hf], qq == 0)
                        S.part("scalar", lambda e, pap=pap, qa=qa: e.activation(
                            out=ETO[:, qa * 128:(qa + 4) * 128], in_=pap[:, 0:512], func=AF.Copy, scale=pcol(PV_OD)),
                            [pb, bPV], bETO[hf], qq == 0, extra=[ete_op])
                    if hf == 0 and pending is not None and SB_PERSIST:
                        salle_evac(pending)
                        pending = None
                    uv = UALLr[:, ft, :, :]

                    def f_sc(e, q0=q0, uv=uv, hf=hf, hh=hh):
                        last = None
                        for q in range(q0, q0 + 16):
                            s = 31 - q
                            for j in range(4):
                                for par in range(2):
                                    et = ETEv if par == 0 else ETOv
                                    cc = hh * 256 + par * 128
                                    last = e.matmul(sbanks[j][0][:, cc:cc + 128],
                                                    lhsT=et[32 * j:32 * j + 32, q, :],
                                                    rhs=uv[32 * j:32 * j + 32, s, :],
                                                    start=(hf == 0 and q == q0 and par == 0),
                                                    stop=(q == 31 and par == 1),
                                                    skip_group_check=True, tile_position=(32 * j, 0))
                        return last
                    if hf == 0:
                        S.op("tensor", f_sc, reads=[bETE[hf], bETO[hf], bU[ft]], writes=[sbuf2[j][hh] for j in range(4)])
                    else:
                        o_ = S.op("tensor", f_sc, reads=[bETE[hf], bETO[hf], bU[ft]])
                        for j in range(4):
                            sbuf2[j][hh].w = [o_]
                pending = ft
                if not SB_PERSIST:
                    salle_evac(pending)
                    pending = None
            if pending is not None:
                salle_evac(pending)
            PS.unhold(sbanks)
            if limit == "A":
                return
            bZa, bZb = Buf(), Buf()
            bZb.w = list(ch.w)
            S.op("vector", lambda e: e.memset(ZA, 0.0), writes=[bZa], reads=[ch])
            bXBF = Buf()
            cur, nxt = ZA, ZB
            bcur, bnxt = bZa, bZb
            for cb in range(NCH // 8):
                pap, pb = PS.get()
                S.op("tensor", lambda e, pap=pap, cb=cb: e.matmul(
                    pap[:, 0:512].rearrange("p (g c) -> p g c", g=64), lhsT=PERM, rhs=SALLv[:, :, cb * 8:(cb + 1) * 8],
                    start=True, stop=True), reads=[bSALL, ch], writes=[pb])
                ssw = pap[:, 0:512].rearrange("p (g c) -> p g c", g=64)
                for ci in range(8):
                    c = cb * 8 + ci
                    if c >= 63:
                        S.part("scalar", lambda e, cur=cur, c=c: e.activation(out=XBFv[:, :, c - 63], in_=cur[:, 0:64], func=AF.Copy),
                               [bcur], bXBF, c == 63)
                    if c == NCH - 1:
                        break

                    def f_step(e, cur=cur, nxt=nxt, c=c, ci=ci, ssw=ssw):
                        e.tensor_tensor(out=ZT1, in0=cur, in1=A1W, op=ALU.mult)
                        e.tensor_tensor(out=ZT2[:, 0:64], in0=cur[:, 64:128], in1=A2W[:, 0:64], op=ALU.mult)
                        e.tensor_tensor(out=ZT2[:, 64:128], in0=cur[:, 0:64], in1=A2W[:, 64:128], op=ALU.mult)
                        e.tensor_tensor(out=ZT1, in0=ZT1, in1=ZT2, op=ALU.add)
                        e.tensor_tensor(out=nxt[:, 0:64], in0=ZT1[:, 0:64], in1=SALLv[:, :, c], op=ALU.add)
                        return e.tensor_tensor(out=nxt[:, 64:128], in0=ZT1[:, 64:128], in1=ssw[:, :, ci], op=ALU.add)
                    S.op("vector", f_step, reads=[pb, bSALL, bcur], writes=[bnxt])
                    cur, nxt = nxt, cur
                    bcur, bnxt = bnxt, bcur

            if limit == "REC":
                return
            o = o_dead
            c1o = o
            C1 = fv(o, 1024); o += 1024
            C2 = fv(o, 1024); o += 1024
            FT32 = fv(o, 33 * 128); o += 4224
            KB = bv(o, 32 * 128); o += 2048
            YST = [bv(o + i * (YW // 2), YW) for i in range(2)]; o += YW
            DD = fv(o, 128); o += 128
            bdmo = o
            BDM = fv(o, 128); o += 128
            ECOL = fv(o, 128); o += 128
            OCOL = fv(o, 128); o += 128
            assert o <= NW, o
            o2 = o_u
            FTT = fv(o2, 4224); o2 += 4224
            FTE = bv(o2, 4096); o2 += 2048
            FTO = bv(o2, 4096); o2 += 2048
            assert o2 <= o_u + 8192 + 26 * 64
            bC = Buf()
            bC.w = bZa.wdeps() + bZb.wdeps() + bXBF.wdeps() + bSALL.wdeps()
            S.op("sync", lambda e: e.dma_start(out=fv(c1o, 2048), in_=s5c_d), writes=[bC], dma="cst")
            S.op("sync", lambda e: e.dma_start(out=fv(bdmo, 384), in_=cst_d[:, 256:640]), writes=[bC], dma="cst")
            S.op("vector", lambda e: e.tensor_scalar(C1, C1, pcol(PV_NSGN), None, ALU.mult), reads=[bPV], writes=[bC])
            S.op("vector", lambda e: e.tensor_scalar(C2, C2, -1.0, None, ALU.mult), writes=[bC])
            bFT32, bFTT, bFTE, bFTO, bKB, bDD = Buf(), Buf(), Buf(), Buf(), Buf(), Buf()
            bFTT.w = list(bC.w)
            bFTE.w = list(bC.w)
            bFTO.w = list(bC.w)
            S.op(POOL, lambda e: e.memset(FTE, 0.0), writes=[bFTE])
            S.op(POOL, lambda e: e.memset(FTO, 0.0), writes=[bFTO])
            bYST = [Buf(), Buf()]
            FT32v = FT32.rearrange("p (s c) -> p s c", s=33)
            FTEv = FTE.rearrange("p (r c) -> p r c", r=32)
            FTOv = FTO.rearrange("p (r c) -> p r c", r=32)
            KBv = KB.rearrange("p (t c) -> p t c", t=32)
            rr = [(0, 7), (7, 14), (14, 21), (21, 28), (28, 32)]
            for ft in range(8):
                pw_r = PWRv[:, :, ft * 8:(ft + 1) * 8].unsqueeze(3).broadcast_to([P, 33, 8, 16])
                pw_i = PWIv[:, :, ft * 8:(ft + 1) * 8].unsqueeze(3).broadcast_to([P, 33, 8, 16])
                a1 = C1[:, ft * 128:(ft + 1) * 128].rearrange("p (g h) -> p g h", g=8).unsqueeze(1).broadcast_to([P, 33, 8, 16])
                a2 = C2[:, ft * 128:(ft + 1) * 128].rearrange("p (g h) -> p g h", g=8).unsqueeze(1).broadcast_to([P, 33, 8, 16])
                f4 = FT32.rearrange("p (s g h) -> p s g h", s=33, g=8)
                t4 = FTT.rearrange("p (s g h) -> p s g h", s=33, g=8)
                S.op("vector", lambda e, f4=f4, a1=a1, pw_r=pw_r: e.tensor_tensor(out=f4, in0=a1, in1=pw_r, op=ALU.mult),
                     reads=[bC, bPW], writes=[bFT32])
                S.op(POOL, lambda e, t4=t4, a2=a2, pw_i=pw_i: e.tensor_tensor(out=t4, in0=a2, in1=pw_i, op=ALU.mult),
                     reads=[bC, bPW], writes=[bFTT])
                S.op("vector", lambda e: e.tensor_tensor(out=FT32, in0=FT32, in1=FTT, op=ALU.add),
                     reads=[bFTT], writes=[bFT32])
                S.op("vector", lambda e, ft=ft: e.tensor_scalar(DD, IDENT, pcol(PV_DS + ft), None, ALU.mult),
                     reads=[bCST, bPV], writes=[bDD])
                for s0 in range(0, 32, 4):
                    pap, pb = PS.get()

                    def f_k(e, pap=pap, s0=s0, ft=ft):
                        last = e.matmul(pap[:, 0:512], lhsT=BST[:, ft * 128:(ft + 1) * 128],
                                        rhs=FT32[:, s0 * 128:(s0 + 4) * 128], start=True, stop=(s0 != 0))
                        if s0 == 0:
                            last = e.matmul(pap[:, 0:128], lhsT=IDENT, rhs=DD, start=False, stop=True, skip_group_check=True)
                        return last
                    S.op("tensor", f_k, reads=[bFT32, bPW, bDD, bCST], writes=[pb])
                    bdb = BDM.unsqueeze(1).broadcast_to([P, 4, 128])
                    S.part("vector", lambda e, pap=pap, s0=s0, bdb=bdb: e.tensor_tensor(
                        out=KBv[:, s0:s0 + 4, :], in0=pap[:, 0:512].rearrange("p (t c) -> p t c", t=4), in1=bdb, op=ALU.mult),
                        [pb, bC], bKB, s0 == 0)
                ecb = ECOL.unsqueeze(1).broadcast_to([P, 32, 128])
                ocb = OCOL.unsqueeze(1).broadcast_to([P, 32, 128])
                f5 = FT32v[:, 1:33, :].rearrange("p r (g two h) -> p r g two h", two=2, h=16)
                S.op("scalar", lambda e, f5=f5: e.activation(
                    out=FTEv.rearrange("p r (g two h) -> p r g two h", two=2, h=16)[:, :, :, 0, :], in_=f5[:, :, :, 0, :], func=AF.Copy),
                    reads=[bFT32], writes=[bFTE])
                S.op("scalar", lambda e, f5=f5: e.activation(
                    out=FTOv.rearrange("p r (g two h) -> p r g two h", two=2, h=16)[:, :, :, 1, :], in_=f5[:, :, :, 1, :], func=AF.Copy),
                    reads=[bFT32], writes=[bFTO])
                uv = UALLr[:, ft, :, :]
                ysl = ft % 2
                ystv = YST[ysl].rearrange("p (c r) -> p c r", r=TC)
                for (ra, rb) in rr:
                    pap, pb = PS.get()

                    def f_y(e, pap=pap, ra=ra, rb=rb, uv=uv, ft=ft):
                        last = None
                        for tau in range(rb):
                            r0 = max(ra, tau)
                            nr = rb - r0
                            outv = pap[:, (r0 - ra) * NCX:(rb - ra) * NCX].rearrange("p (r c) -> p r c", r=nr)
                            rhs = uv[:, r0 - tau:rb - tau, 63:128]
                            last = e.matmul(outv, lhsT=KBv[:, tau, :], rhs=rhs, start=(tau == 0), stop=False,
                                            skip_group_check=True)
                        for r in range(ra, rb):
                            for j in range(4):
                                for par in range(2):
                                    g = ft * 8 + 2 * j + par
                                    fts = FTEv if par == 0 else FTOv
                                    last = e.matmul(pap[32 * j:32 * j + 32, (r - ra) * NCX:(r - ra + 1) * NCX],
                                                    lhsT=fts[:, r, 32 * j:32 * j + 32], rhs=XBFv[:, g, :],
                                                    start=False, stop=(r == rb - 1 and j == 3 and par == 1),
                                                    skip_group_check=True, tile_position=(0, 32 * j))
                        return last
                    S.op("tensor", f_y, reads=[bKB, bFTE, bFTO, bXBF, bU[ft]], writes=[pb])
                    nr = rb - ra
                    S.part("scalar", lambda e, pap=pap, ra=ra, rb=rb, nr=nr, ystv=ystv: e.activation(
                        out=ystv[:, :, ra:rb].rearrange("p c r -> p r c"),
                        in_=pap[:, 0:nr * NCX].rearrange("p (r c) -> p r c", r=nr), func=AF.Gelu_apprx_tanh),
                        [pb], bYST[ysl], ra == 0)
                S.op("sync", lambda e, ft=ft, ysl=ysl: e.dma_start(out=ya0_d[:, ft * YW:(ft + 1) * YW], in_=YST[ysl]),
                     reads=[bYST[ysl]], dma="ya", serial=False)
            ya_done = S.q["sync"][-1]
            if limit == "B":
                S.op("sync", lambda e: e.nop(), extra=[ya_done])
                return
            phaseU_bufs = [bFT32, bFTT, bFTE, bFTO, bKB, bDD, bXBF, bC, ch, bYST[0], bYST[1]] + bU

            o = BASE
            R1 = bv(o, 16 * NCOL); o += 8208
            R2o = o
            R2 = bv(o, 16 * NCOL); o += 8208
            R3o = o
            R3 = fv(o, 16 * NCOL); o += 16416
            SMo = o
            o += 2100
            assert o <= NW, o
            R1v = R1.rearrange("p (k n) -> p k n", k=16)
            R2v = R2.rearrange("p (k n) -> p k n", k=16)
            R3v = R3.rearrange("p (k n) -> p k n", k=16)
            YA0P = bv(R3o, 8 * NCOL).rearrange("p (k n) -> p k n", k=8)
            YAIN = bv(R3o + 4104, 8 * NCOL).rearrange("p (k n) -> p k n", k=8)
            YBIN = bv(R3o + 8208, 8 * NCOL).rearrange("p (k n) -> p k n", k=8)
            SPo = R3o + 12312
            XST = [fv(SMo, NCOL), fv(SMo + NCOL, NCOL)]
            GB = [bv(R2o + i * 4104, 8 * NCOL).rearrange("p (k n) -> p k n", k=8) for i in range(1)]
            EXT = fv(SMo, 1028)
            TT = fv(SMo + 1028, NCOL)
            OST = [fv(R2o + 4104 + i * NCOL, NCOL) for i in range(2)]

            bR1 = [Buf() for _ in range(16)]
            bR2 = [Buf() for _ in range(16)]
            bR3 = [Buf() for _ in range(16)]
            first_deps = []
            for b in phaseU_bufs:
                first_deps += b.wdeps()
            for b in bR1 + bR2 + bR3:
                b.w = list(first_deps)
            bXST = [Buf(), Buf()]
            for b in bXST:
                b.w = list(first_deps)
            for b in W.bufs:
                b.w = list(first_deps)
            bRSTD = Buf()
            xst_i = [0]

            def load_x(j, c0):
                sl = xst_i[0] % 2
                xst_i[0] += 1
                S.op("sync", lambda e, sl=sl, j=j, c0=c0: e.dma_start(out=XST[sl], in_=xT[j * P:(j + 1) * P, c0:c0 + NCOL]),
                     writes=[bXST[sl]], dma="xs%d" % sl)
                return sl

            def stats_and_rstd(sqv, bsq, sd_ap, bsd):
                banks = [PS.get() for _ in range(NBLK)]

                def f_st(e):
                    last = None
                    for k in range(16):
                        for b in range(NBLK):
                            last = e.matmul(banks[b][0][:, 0:BW], lhsT=ONES, rhs=sqv[:, k, b * BW:(b + 1) * BW],
                                            start=(k == 0), stop=(k == 15))
                    return last
                S.op("tensor", f_st, reads=list(bsq) + [bONES], writes=[bk[1] for bk in banks])
                for b in range(NBLK):
                    S.op("scalar", lambda e, b=b: e.activation(out=sd_ap[:, b * BW:(b + 1) * BW], in_=banks[b][0][:, 0:BW],
                                                                func=AF.Ln, bias=EPS, scale=1.0 / D),
                         reads=[banks[b][1]], writes=[], extra=bsd.wdeps() if b == 0 else bsd.w)
                    if b == 0:
                        bsd.w = [S.q["scalar"][-1]]
                        bsd.r = {}
                    else:
                        bsd.w.append(S.q["scalar"][-1])
                S.op("scalar", lambda e: e.activation(out=RSTD, in_=sd_ap, func=AF.Exp, scale=-0.5), reads=[bsd], writes=[bRSTD])

            def mm_tile(lhs_fn, rhs_fn, nk, reads):
                banks = [PS.get() for _ in range(NBLK)]

                def f(e):
                    last = None
                    for k in range(nk):
                        for b in range(NBLK):
                            last = e.matmul(banks[b][0][:, 0:BW], lhsT=lhs_fn(k), rhs=rhs_fn(k, b),
                                            start=(k == 0), stop=(k == nk - 1))
                    return last
                S.op("tensor", f, reads=reads, writes=[bk[1] for bk in banks])
                return banks

            def multi_write(buf, eng, first):
                o_ = S.q[eng][-1]
                if first:
                    buf.w = [o_]
                    buf.r = {}
                else:
                    buf.w.append(o_)

            xn_prefetched = [False]
            for ps_ in range(2):
                c0 = 2044 + ps_ * NCOL
                yc0 = 28 + ps_ * NCOL
                bYA0 = Buf()
                bYA0.w = bR3[0].wdeps()
                for j in range(1, 12):
                    bYA0.w += bR3[j].wdeps()
                S.op("sync", lambda e, yc0=yc0: e.dma_start(
                    out=YA0P, in_=ya0_d.rearrange("p (f n) -> p f n", f=8)[:, :, yc0:yc0 + NCOL]),
                    writes=[bYA0], extra=[ya_done], dma="yaL")
                bYAIN = [Buf() for _ in range(8)]
                bYBIN = [Buf() for _ in range(8)]
                for b in bYAIN + bYBIN:
                    b.w = list(bYA0.w[:-1])
                bSP = Buf()
                bSP.w = list(bYA0.w[:-1])
                bSD1 = Buf()
                bSD1.w = list(bSP.w)
                if not xn_prefetched[0]:
                    XS4a = [fv(R2o + i * NCOL, NCOL) for i in range(4)]
                    bXS4a = [Buf() for _ in range(4)]
                    for b in bXS4a:
                        for b2 in bR2:
                            b.w += b2.wdeps()
                    for j in range(16):
                        s4 = j % 4
                        S.op("sync", lambda e, s4=s4, j=j, c0=c0: e.dma_start(out=XS4a[s4], in_=xT[j * P:(j + 1) * P, c0:c0 + NCOL]),
                             writes=[bXS4a[s4]], dma="xq%d" % s4)
                        S.op("vector", lambda e, s4=s4, j=j, ps_=ps_: e.scalar_tensor_tensor(
                            out=R1v[:, j, :], in0=XS4a[s4], scalar=pcol(PV_NT + j), in1=RSA[:, ps_ * NCOL:(ps_ + 1) * NCOL],
                            op0=ALU.mult, op1=ALU.mult),
                            reads=[bXS4a[s4], bRSA, bPV], writes=[bR1[j]])
                    for b2 in bR2:
                        for b in bXS4a:
                            b2.w = b2.wdeps() + b.wdeps()
                        b2.r = {}
                if ps_ == 0:
                    dump("d_xn", R1, bR1)
                VT = [fv(SPo + i * NCOL, NCOL) for i in range(2)]
                CT = [fv(SPo + 2052 + i * NCOL, NCOL) for i in range(2)]
                EXTm = [fv(SMo, 1028), fv(SMo + 1028, 1028)]
                bVT = [Buf(), Buf()]
                bCT = [Buf(), Buf()]
                bEXTm = [Buf(), Buf()]
                for b in bVT + bCT:
                    b.w = bSD1.wdeps() + [S.q["vector"][-1]]
                for b in bEXTm:
                    b.w = bXST[0].wdeps() + bXST[1].wdeps()
                for ip in range(4):
                    wv, wb_, wk = W.get(w_in, 0, 16, 1024 + ip * 256, 256)
                    for t in range(2):
                        banks = mm_tile(lambda k, wv=wv, t=t: wv[:, k, t * 128:(t + 1) * 128],
                                        lambda k, b: R1v[:, k, b * BW:(b + 1) * BW], 16, bR1 + [wb_])
                        for b in range(NBLK):
                            S.op("scalar", lambda e, t=t, b=b, banks=banks: e.activation(
                                out=VT[t][:, b * BW:(b + 1) * BW], in_=banks[b][0][:, 0:BW], func=AF.Copy),
                                reads=[banks[b][1]], writes=[], extra=bVT[t].wdeps() if b == 0 else bVT[t].w)
                            multi_write(bVT[t], "scalar", b == 0)
                    W.release(wk)
                    wv, wb_, wk = W.get(w_in, 0, 16, 3072 + ip * 256, 256)
                    for t in range(2):
                        i = ip * 2 + t
                        banks = mm_tile(lambda k, wv=wv, t=t: wv[:, k, t * 128:(t + 1) * 128],
                                        lambda k, b: R1v[:, k, b * BW:(b + 1) * BW], 16, bR1 + [wb_])
                        S.op(POOL, lambda e, t=t, i=i: e.tensor_copy(out=EXTm[t][:, 0:2], in_=CVC[:, 2 * i:2 * i + 2]),
                             reads=[bCVC], writes=[bEXTm[t]])
                        for b in range(NBLK):
                            S.op("vector", lambda e, t=t, b=b, banks=banks: e.tensor_tensor(
                                out=EXTm[t][:, 2 + b * BW:2 + (b + 1) * BW], in0=VT[t][:, b * BW:(b + 1) * BW],
                                in1=banks[b][0][:, 0:BW], op=ALU.mult),
                                reads=[banks[b][1], bVT[t]], writes=[], extra=bEXTm[t].w)
                            bEXTm[t].w.append(S.q["vector"][-1])
                        S.op(POOL, lambda e, t=t, i=i: e.tensor_copy(out=CVC[:, 2 * i:2 * i + 2], in_=EXTm[t][:, NCOL:NCOL + 2]),
                             reads=[bEXTm[t]], writes=[bCVC])

                        def f_conv(e, t=t, i=i):
                            e.tensor_scalar(CT[t], EXTm[t][:, 0:NCOL], pcol(PV_CW + i), None, ALU.mult)
                            e.scalar_tensor_tensor(out=CT[t], in0=EXTm[t][:, 1:NCOL + 1], scalar=pcol(PV_CW + 8 + i),
                                                   in1=CT[t], op0=ALU.mult, op1=ALU.add)
                            return e.scalar_tensor_tensor(out=CT[t], in0=EXTm[t][:, 2:NCOL + 2], scalar=pcol(PV_CW + 16 + i),
                                                          in1=CT[t], op0=ALU.mult, op1=ALU.add)
                        S.op("vector", f_conv, reads=[bEXTm[t], bPV], writes=[bCT[t]])
                    W.release(wk)
                    wv, wb_, wk = W.get(w_in, 0, 16, 2048 + ip * 256, 256)
                    for t in range(2):
                        i = ip * 2 + t
                        banks = mm_tile(lambda k, wv=wv, t=t: wv[:, k, t * 128:(t + 1) * 128],
                                        lambda k, b: R1v[:, k, b * BW:(b + 1) * BW], 16, bR1 + [wb_])
                        for b in range(NBLK):
                            S.op("vector", lambda e, t=t, b=b, i=i, banks=banks: e.scalar_tensor_tensor(
                                out=YBIN[:, i, b * BW:(b + 1) * BW], in0=CT[t][:, b * BW:(b + 1) * BW],
                                scalar=pcol(PV_CB + i), in1=banks[b][0][:, 0:BW], op0=ALU.add, op1=ALU.mult),
                                reads=[banks[b][1], bCT[t], bPV], writes=[], extra=bYBIN[i].wdeps() if b == 0 else bYBIN[i].w)
                            multi_write(bYBIN[i], "vector", b == 0)
                    W.release(wk)
                SG = [fv(SPo + i * NCOL, NCOL) for i in range(4)]
                bSG = [Buf() for _ in range(4)]
                for b in bSG:
                    b.w = bVT[0].wdeps() + bVT[1].wdeps() + bCT[0].wdeps() + bCT[1].wdeps()
                for ih in range(2):
                    wv, wb_, wk = W.get(w_glu, 0, 8, ih * 512, 512)
                    for t in range(4):
                        i = ih * 4 + t
                        banks = mm_tile(lambda k, wv=wv, t=t: wv[:, k, t * 128:(t + 1) * 128],
                                        lambda k, b: YA0P[:, k, b * BW:(b + 1) * BW], 8, [bYA0, wb_])
                        sgi = i % 4
                        for b in range(NBLK):
                            S.op("scalar", lambda e, b=b, sgi=sgi, banks=banks: e.activation(
                                out=SG[sgi][:, b * BW:(b + 1) * BW], in_=banks[b][0][:, 0:BW], func=AF.Sigmoid),
                                reads=[banks[b][1]], writes=[], extra=bSG[sgi].wdeps() if b == 0 else bSG[sgi].w)
                            multi_write(bSG[sgi], "scalar", b == 0)
                        S.op("vector", lambda e, i=i, sgi=sgi: e.tensor_tensor(out=YAIN[:, i, :], in0=SG[sgi], in1=YA0P[:, i, :], op=ALU.mult),
                             reads=[bSG[sgi], bYA0], writes=[bYAIN[i]])
                    W.release(wk)
                if ps_ == 0:
                    dump("d_yb", bv(R3o + 8208, 8 * NCOL), bYBIN)
                    dump("d_ya", bv(R3o + 4104, 8 * NCOL), bYAIN)
                for jp in range(8):
                    j0 = jp * 2
                    wma, bma, wk = W.get(w_in, 0, 16, 4096 + jp * 256, 256)
                    for t in range(2):
                        ba, bba = SG[t], bSG[t]
                        banks = mm_tile(lambda k, t=t, wma=wma: wma[:, k, t * 128:(t + 1) * 128],
                                        lambda k, b: R1v[:, k, b * BW:(b + 1) * BW], 16, bR1 + [bma])
                        for b in range(NBLK):
                            S.part("scalar", lambda e, b=b, ba=ba, banks=banks: e.activation(
                                out=ba[:, b * BW:(b + 1) * BW], in_=banks[b][0][:, 0:BW], func=AF.Sigmoid),
                                [banks[b][1]], bba, b == 0)
                    W.release(wk)
                    if ps_ == 0 and jp == 0:
                        dump("d_sa", SG[0], [bSG[0]])
                    wso, bso, wk = W.get(w_so, 0, 8, jp * 256, 256)
                    for t in range(2):
                        ba, bba = SG[t], bSG[t]
                        banks = mm_tile(lambda k, t=t, wso=wso: wso[:, k, t * 128:(t + 1) * 128],
                                        lambda k, b: YAIN[:, k, b * BW:(b + 1) * BW], 8, bYAIN + [bso])
                        for b in range(NBLK):
                            S.part("vector", lambda e, b=b, ba=ba, banks=banks: e.tensor_tensor(
                                out=ba[:, b * BW:(b + 1) * BW], in0=ba[:, b * BW:(b + 1) * BW], in1=banks[b][0][:, 0:BW], op=ALU.mult),
                                [banks[b][1]], bba, False, extra=list(bba.r.values()))
                    W.release(wk)
                    wmb, bmb, wk = W.get(w_in, 0, 16, 6144 + jp * 256, 256)
                    for t in range(2):
                        bb, bbb = SG[2 + t], bSG[2 + t]
                        banks = mm_tile(lambda k, t=t, wmb=wmb: wmb[:, k, t * 128:(t + 1) * 128],
                                        lambda k, b: R1v[:, k, b * BW:(b + 1) * BW], 16, bR1 + [bmb])
                        for b in range(NBLK):
                            S.part("scalar", lambda e, b=b, bb=bb, banks=banks: e.activation(
                                out=bb[:, b * BW:(b + 1) * BW], in_=banks[b][0][:, 0:BW], func=AF.Sigmoid),
                                [banks[b][1]], bbb, b == 0)
                    W.release(wk)
                    wco, bco, wk = W.get(w_co, 0, 8, jp * 256, 256)
                    for t in range(2):
                        bb, bbb = SG[2 + t], bSG[2 + t]
                        banks = mm_tile(lambda k, t=t, wco=wco: wco[:, k, t * 128:(t + 1) * 128],
                                        lambda k, b: YBIN[:, k, b * BW:(b + 1) * BW], 8, bYBIN + [bco])
                        for b in range(NBLK):
                            S.part("vector", lambda e, b=b, bb=bb, banks=banks: e.tensor_tensor(
                                out=bb[:, b * BW:(b + 1) * BW], in0=bb[:, b * BW:(b + 1) * BW], in1=banks[b][0][:, 0:BW], op=ALU.mult),
                                [banks[b][1]], bbb, False, extra=list(bbb.r.values()))
                    W.release(wk)
                    for t in range(2):
                        j = j0 + t
                        if ps_ == 0 and j == 0:
                            dump("d_ba", SG[0], [bSG[0]])
                            dump("d_bb", SG[2], [bSG[2]])
                        S.op(POOL, lambda e, j=j, t=t: e.tensor_tensor(out=R2v[:, j, :], in0=SG[t], in1=SG[2 + t], op=ALU.add),
                             reads=[bSG[t], bSG[2 + t]], writes=[bR2[j]])
                if ps_ == 0:
                    dump("d_mg", R2, bR2)
                phaseM = [bYA0] + bYAIN + bYBIN + bSG + bVT + bCT
                r3guard = []
                for b in phaseM:
                    r3guard += b.wdeps()
                for b in bR3:
                    b.w = list(r3guard)
                    b.r = {}
                for b in bXST:
                    b.w = b.wdeps() + bEXTm[0].wdeps() + bEXTm[1].wdeps()
                    b.r = {}
                for jp in range(8):
                    wv, wb_, wk = W.get(w_o, 0, 16, jp * 256, 256)
                    for t in range(2):
                        j = jp * 2 + t
                        banks = mm_tile(lambda k, t=t, wv=wv: wv[:, k, t * 128:(t + 1) * 128],
                                        lambda k, b: R2v[:, k, b * BW:(b + 1) * BW], 16, bR2 + [wb_])
                        sl = load_x(j, c0)
                        for b in range(NBLK):
                            S.op("vector", lambda e, b=b, j=j, sl=sl, banks=banks: e.tensor_tensor(
                                out=R3v[:, j, b * BW:(b + 1) * BW], in0=XST[sl][:, b * BW:(b + 1) * BW], in1=banks[b][0][:, 0:BW], op=ALU.add),
                                reads=[banks[b][1], bXST[sl]], writes=[], extra=bR3[j].wdeps() if b == 0 else bR3[j].w)
                            multi_write(bR3[j], "vector", b == 0)
                        S.op("scalar", lambda e, j=j: e.activation(out=R1v[:, j, :], in_=R3v[:, j, :], func=AF.Square),
                             reads=[bR3[j]], writes=[bR1[j]])
                    W.release(wk)
                if ps_ == 0:
                    dump("d_h1", R3, bR3)
                SD2 = fv(R2o, NCOL)
                bSD2 = Buf()
                for b in bR2:
                    bSD2.w += b.wdeps()
                stats_and_rstd(R1v, bR1, SD2, bSD2)
                for j in range(16):
                    S.op("vector", lambda e, j=j: e.scalar_tensor_tensor(
                        out=R1v[:, j, :], in0=R3v[:, j, :], scalar=pcol(PV_NF + j), in1=RSTD, op0=ALU.mult, op1=ALU.mult),
                        reads=[bR3[j], bRSTD, bPV], writes=[bR1[j]])
                if ps_ == 0:
                    dump("d_hn", R1, bR1)
                bG = Buf()
                bG.w = bSD2.wdeps()
                bEXT, bTT = Buf(), Buf()
                bEXT.w = bXST[0].wdeps() + bXST[1].wdeps()
                bTT.w = list(bEXT.w)
                groups = [(0, 4), (4, 12), (12, 20), (20, 28), (28, 36), (36, 44)]
                for (fa, fb) in groups:
                    ng = fb - fa
                    bGf = [Buf() for _ in range(ng)]
                    for b in bGf:
                        b.w = bG.wdeps()
                    for fp in range(fa, fb, 2):
                        wa, bwa, wka = W.get(w_up, 0, 16, fp * 128, 256)
                        wbb, bwb, wkb = W.get(w_up, 0, 16, DFF + fp * 128, 256)
                        for t in range(2):
                            f = fp + t
                            fl = f - fa
                            banksA = mm_tile(lambda k, t=t, wa=wa: wa[:, k, t * 128:(t + 1) * 128],
                                             lambda k, b: R1v[:, k, b * BW:(b + 1) * BW], 16, bR1 + [bwa])
                            banksB = mm_tile(lambda k, t=t, wbb=wbb: wbb[:, k, t * 128:(t + 1) * 128],
                                             lambda k, b: R1v[:, k, b * BW:(b + 1) * BW], 16, bR1 + [bwb])
                            S.op(POOL, lambda e, f=f: e.tensor_copy(out=EXT[:, 0:2], in_=HAC[:, 2 * f:2 * f + 2]),
                                 reads=[bHAC], writes=[bEXT])
                            for b in range(NBLK):
                                S.op("scalar", lambda e, b=b, banksA=banksA: e.activation(
                                    out=EXT[:, 2 + b * BW:2 + (b + 1) * BW], in_=banksA[b][0][:, 0:BW], func=AF.Copy),
                                    reads=[banksA[b][1]], writes=[], extra=bEXT.w)
                                bEXT.w.append(S.q["scalar"][-1])
                            S.op(POOL, lambda e, f=f: e.tensor_copy(out=HAC[:, 2 * f:2 * f + 2], in_=EXT[:, NCOL:NCOL + 2]),
                                 reads=[bEXT], writes=[bHAC])

                            def f_conv2(e, f=f):
                                e.tensor_scalar(TT, EXT[:, 0:NCOL], pcol(PV_FW + f), None, ALU.mult)
                                e.scalar_tensor_tensor(out=TT, in0=EXT[:, 1:NCOL + 1], scalar=pcol(PV_FW + 44 + f),
                                                       in1=TT, op0=ALU.mult, op1=ALU.add)
                                return e.scalar_tensor_tensor(out=TT, in0=EXT[:, 2:NCOL + 2], scalar=pcol(PV_FW + 88 + f),
                                                              in1=TT, op0=ALU.mult, op1=ALU.add)
                            S.op("vector", f_conv2, reads=[bEXT, bPV], writes=[bTT])
                            S.op("scalar", lambda e, f=f: e.activation(out=TT, in_=TT, func=AF.Gelu_apprx_tanh, bias=pcol(PV_FB + f)),
                                 reads=[bPV], writes=[bTT])
                            for b in range(NBLK):
                                S.op("vector", lambda e, b=b, fl=fl, banksB=banksB: e.tensor_tensor(
                                    out=GB[0][:, fl, b * BW:(b + 1) * BW], in0=TT[:, b * BW:(b + 1) * BW], in1=banksB[b][0][:, 0:BW], op=ALU.mult),
                                    reads=[banksB[b][1], bTT], writes=[], extra=bGf[fl].wdeps() if b == 0 else bGf[fl].w)
                                multi_write(bGf[fl], "vector", b == 0)
                        W.release(wka)
                        W.release(wkb)
                    last_grp = (fb == FT_FF)
                    if last_grp:
                        stb = PS.hold_specific([5, 6, 7]) if os.environ.get("K_NOHOLD", "0") != "1" else [PS.get() + (0,) for _ in range(3)]
                        SQR = [bv(R2o + 6156 + i * 513, NCOL) for i in range(2)]
                        bSQR = [Buf(), Buf()]
                        for b in bSQR:
                            for b2 in bR2:
                                b.w += b2.wdeps()
                        pend = []
                        for b in bXST:
                            b.w = b.wdeps() + bEXT.wdeps() + bTT.wdeps()
                            b.r = {}

                        def stat_acc(j):
                            sl = j % 2

                            def f(e, j=j, sl=sl):
                                last = None
                                for b in range(NBLK):
                                    last = e.matmul(stb[b][0][:, 0:BW], lhsT=ONES, rhs=SQR[sl][:, b * BW:(b + 1) * BW],
                                                    start=(j == 0 or NOACC), stop=(j == 15 or NOACC), skip_group_check=True)
                                return last
                            S.op("tensor", f, reads=[bSQR[sl], bONES], writes=[sb_[1] for sb_ in stb])
                    for jq in range(4):
                        wd, bwd, wkd = W.get(w_dn, fa * P, ng, jq * 512, 512)
                        for t in range(4):
                            j = jq * 4 + t
                            banks = mm_tile(lambda k, t=t, wd=wd: wd[:, k, t * 128:(t + 1) * 128],
                                            lambda k, b: GB[0][:, k, b * BW:(b + 1) * BW], ng, bGf + [bwd])
                            if last_grp and ps_ == 0 and PF:
                                c1 = 2044 + NCOL
                                sl = j % 2
                                S.op("sync", lambda e, sl=sl, j=j, c1=c1: e.dma_start(out=XST[sl], in_=xT[j * P:(j + 1) * P, c1:c1 + NCOL]),
                                     writes=[bXST[sl]], dma="xs%d" % sl)
                            for b in range(NBLK):
                                S.op("vector", lambda e, b=b, j=j, banks=banks: e.tensor_tensor(
                                    out=R3v[:, j, b * BW:(b + 1) * BW], in0=R3v[:, j, b * BW:(b + 1) * BW], in1=banks[b][0][:, 0:BW], op=ALU.add),
                                    reads=[banks[b][1], bR3[j]], writes=[])
                                bR3[j].w.append(S.q["vector"][-1])
                            if last_grp:
                                if len(pend) >= 2:
                                    stat_acc(pend.pop(0))
                                S.op("scalar", lambda e, j=j: e.activation(out=SQR[j % 2], in_=R3v[:, j, :], func=AF.Square),
                                     reads=[bR3[j]], writes=[bSQR[j % 2]])
                                pend.append(j)
                                if ps_ == 0 and j >= 1 and PF:
                                    jj = j - 1
                                    S.op("vector", lambda e, jj=jj: e.scalar_tensor_tensor(
                                        out=R1v[:, jj, :], in0=XST[jj % 2], scalar=pcol(PV_NT + jj), in1=RSA[:, NCOL:2 * NCOL],
                                        op0=ALU.mult, op1=ALU.mult),
                                        reads=[bXST[jj % 2], bRSA, bPV], writes=[bR1[jj]])
                        W.release(wkd)
                    if last_grp:
                        while pend:
                            stat_acc(pend.pop(0))
                        if ps_ == 0 and PF:
                            S.op("vector", lambda e: e.scalar_tensor_tensor(
                                out=R1v[:, 15, :], in0=XST[1], scalar=pcol(PV_NT + 15), in1=RSA[:, NCOL:2 * NCOL],
                                op0=ALU.mult, op1=ALU.mult),
                                reads=[bXST[1], bRSA, bPV], writes=[bR1[15]])
                            xn_prefetched[0] = True
                    bG.w = []
                    bG.r = {}
                    for b in bGf:
                        bG.w += b.wdeps()
                if ps_ == 0:
                    dump("d_h2", R3, bR3)
                SD3 = fv(R2o, NCOL)
                bSD3 = Buf()
                bSD3.w = bG.wdeps()
                for b in range(NBLK):
                    S.part("scalar", lambda e, b=b: e.activation(out=SD3[:, b * BW:(b + 1) * BW], in_=stb[b][0][:, 0:BW],
                                                                func=AF.Ln, bias=EPS, scale=1.0 / D),
                           [stb[b][1]], bSD3, b == 0)
                if os.environ.get("K_NOHOLD", "0") != "1":
                    PS.unhold(stb)
                S.op("scalar", lambda e: e.activation(out=RSTD, in_=SD3, func=AF.Exp, scale=-0.5), reads=[bSD3], writes=[bRSTD])
                bOST = [Buf(), Buf()]
                for b in bOST:
                    b.w = bG.wdeps()
                bXS4 = []

                def out_tile(j):
                    sl = j % 2
                    S.op("vector", lambda e, j=j, sl=sl: e.scalar_tensor_tensor(
                        out=OST[sl], in0=R3v[:, j, :], scalar=pcol(PV_NL + j), in1=RSTD, op0=ALU.mult, op1=ALU.mult),
                        reads=[bR3[j], bRSTD, bPV], writes=[bOST[sl]])
                    if ps_ == 0:
                        S.op("sync", lambda e, j=j, sl=sl: e.dma_start(out=outT[j * P:(j + 1) * P, 0:NCOL - 4], in_=OST[sl][:, 4:NCOL]),
                             reads=[bOST[sl]], dma="out", serial=False)
                    else:
                        S.op("sync", lambda e, j=j, sl=sl: e.dma_start(out=outT[j * P:(j + 1) * P, NCOL - 4:TOK], in_=OST[sl]),
                             reads=[bOST[sl]], dma="out", serial=False)
                for j in (12, 13, 14, 15) + tuple(range(12)):
                    out_tile(j)
                for b in bXST:
                    b.w = b.wdeps() + [S.q["vector"][-1]]
                    b.r = {}
                for b in bR2:
                    b.w = b.wdeps() + bOST[0].wdeps() + bOST[1].wdeps() + bG.wdeps() + bSD3.wdeps() + bSQR[0].wdeps() + bSQR[1].wdeps()
                    b.r = {}
            last_out = S.q["sync"][-1]
            S.op("sync", lambda e: e.nop(), extra=[last_out])


        S0 = Sched()
        W0 = WeightStream(S0, slots, plan=None)
        program(S0, W0)
        S = Sched()
        W = WeightStream(S, slots, plan=W0.log)
        program(S, W)
        assert W.issued == len(W0.log), (W.issued, len(W0.log))
        S.finalize()

        @block.sync
        def _(e):
            S.emit("sync", e, sems)

        @block.gpsimd
        def _(e):
            S.emit("gpsimd", e, sems)

        @block.tensor
        def _(e):
            S.emit("tensor", e, sems)

        @block.scalar
        def _(e):
            S.emit("scalar", e, sems)

        @block.vector
        def _(e):
            S.emit("vector", e, sems)
    return nc


def _cols(v):
    v = np.asarray(v, np.float32).reshape(-1)
    n = v.size // P
    return np.ascontiguousarray(v.reshape(n, P).T)


def _prep_shared(inp):
    pv = np.zeros((P, NPV), np.float32)
    pv[:, PV_NT:PV_NT + 16] = _cols(inp["norm_tok"][0])
    pv[:, PV_NF:PV_NF + 16] = _cols(inp["norm_ffn"][0])
    pv[:, PV_NL:PV_NL + 16] = _cols(inp["norm_final"])
    for k in range(3):
        pv[:, PV_CW + 8 * k:PV_CW + 8 * k + 8] = _cols(inp["conv_w"][0, k])
        pv[:, PV_FW + 44 * k:PV_FW + 44 * k + 44] = _cols(inp["ffn_conv_w"][0, k])
    pv[:, PV_CB:PV_CB + 8] = _cols(inp["conv_b"][0])
    pv[:, PV_FB:PV_FB + 44] = _cols(inp["ffn_conv_b"][0])
    pv[:, PV_DS:PV_DS + 8] = _cols(inp["d_skip"][0])
    pv[:64, PV_SGN] = -1.0
    pv[64:, PV_SGN] = 1.0
    pv[:64, PV_NSGN] = 1.0
    pv[64:, PV_NSGN] = -1.0
    gl = (np.arange(P) // 16) % 2
    pv[:, PV_EV] = (gl == 0)
    pv[:, PV_OD] = (gl == 1)
    arT = np.asarray(inp["a_re"][0], np.float32).T
    aiT = np.asarray(inp["a_im"][0], np.float32).T
    ldt = np.broadcast_to(np.asarray(inp["log_dt"][0], np.float32)[None, :], (P, 64))
    s5a = np.concatenate([np.concatenate([arT, arT], 0), np.concatenate([aiT, aiT], 0), ldt], 1)
    br = np.asarray(inp["b_re"][0], np.float32).transpose(1, 0, 2).reshape(64, 1024)
    bi = np.asarray(inp["b_im"][0], np.float32).transpose(1, 0, 2).reshape(64, 1024)
    cr = np.asarray(inp["c_re"][0], np.float32).transpose(2, 0, 1).reshape(64, 1024)
    ci = np.asarray(inp["c_im"][0], np.float32).transpose(2, 0, 1).reshape(64, 1024)
    s5b = np.concatenate([np.concatenate([br, bi], 0), np.concatenate([bi, br], 0)], 1)
    s5c = np.concatenate([np.concatenate([cr, ci], 0), np.concatenate([ci, cr], 0)], 1)
    ident = np.eye(P, dtype=np.float32)
    perm = np.roll(ident, 64, axis=0)
    blk = np.arange(P) // 16
    bdm = (blk[:, None] == blk[None, :]).astype(np.float32)
    ecol = np.broadcast_to((gl == 0).astype(np.float32)[None, :], (P, P))
    ocol = np.broadcast_to((gl == 1).astype(np.float32)[None, :], (P, P))
    cst = np.concatenate([ident, perm, bdm, ecol, ocol], 1)
    sh = {
        "w_in": np.ascontiguousarray(inp["w_in"][0], np.float32),
        "w_glu": np.ascontiguousarray(inp["w_glu"][0], np.float32),
        "w_ssm_out": np.ascontiguousarray(inp["w_ssm_out"][0], np.float32),
        "w_conv_out": np.ascontiguousarray(inp["w_conv_out"][0], np.float32),
        "w_o": np.ascontiguousarray(inp["w_o"][0], np.float32),
        "w_up": np.ascontiguousarray(inp["w_up"][0], np.float32),
        "w_down": np.ascontiguousarray(inp["w_down"][0], np.float32),
        "pvec": pv,
        "s5a": np.ascontiguousarray(s5a, np.float32),
        "s5b": np.ascontiguousarray(s5b, np.float32),
        "s5c": np.ascontiguousarray(s5c, np.float32),
        "cst": np.ascontiguousarray(cst, np.float32),
    }
    return sh


def _in_maps(inp, cores):
    sh = _prep_shared(inp)
    x = np.asarray(inp["x"], np.float32)
    maps = []
    for i in cores:
        b, half = i // 2, i % 2
        xa = np.zeros((2 * TOK, D), np.float32)
        if half == 1:
            xa[:] = x[b]
        else:
            xa[TOK:] = x[b, :TOK]
        m = dict(sh)
        m["xT"] = np.ascontiguousarray(xa.T)
        maps.append(m)
    return maps


_NC_CACHE = {}


def kernel(**inputs):
    if "nc" not in _NC_CACHE:
        _NC_CACHE["nc"] = build()
    nc = _NC_CACHE["nc"]
    maps = _in_maps(inputs, list(range(8)))
    res = run_bass_kernel_spmd(nc, maps, core_ids=list(range(8)))
    out = np.empty((4, 2 * TOK, D), np.float32)
    for i in range(8):
        b, half = i // 2, i % 2
        out[b, half * TOK:(half + 1) * TOK, :] = res.results[i]["outT"].T
    return out
```
